# Optimizing a Trainium2 kernel written in Bass

```python
import jax, jax.numpy as jnp
from jax import lax
import numpy as np

D_MODEL = 1024
BATCH = 32
SEQ = 256
DEPTH = 4
DEC_BATCH = 8
DEC_SEQ = 2048
PAST_LEN = 256

GRID_W = 64
N_HEADS = 16
N_KV_HEADS = 4
HEAD_DIM = 64
Q_PER_KV = N_HEADS // N_KV_HEADS
ATTN_WIDTH = N_HEADS * HEAD_DIM
KV_WIDTH = N_KV_HEADS * HEAD_DIM
N_FOURIER_GROUPS = 4
FOURIER_GROUP_DIM = 128
FOURIER_WIDTH = N_FOURIER_GROUPS * FOURIER_GROUP_DIM
N_BRANCHES = 2
IN_WIDTH = FOURIER_WIDTH + ATTN_WIDTH + 2 * KV_WIDTH + N_BRANCHES * D_MODEL
WINDOW = 128
BLOCK = 128
D_FF = 4 * D_MODEL
ROPE_THETA = 10000.0
ROPE_PAIRS_PER_AXIS = HEAD_DIM // 4
EPS = 1e-6
NEG_INF = -1e30

kernel_name = "hybrid_fnet_swa_prefix_diffusion_step"


def rmsnorm(x, g):
    xf = x.astype(jnp.float32)
    y = xf * lax.rsqrt(jnp.mean(xf * xf, axis=-1, keepdims=True) + EPS)
    return (y * g.astype(jnp.float32)).astype(x.dtype)


def modulation(cvec, w_ada, b_ada):
    m = jax.nn.silu(cvec) @ w_ada + b_ada
    return [t[:, None, :] for t in jnp.split(m, 6, axis=-1)]


def axial_rope_tables(rows, n_tokens):
    t = jnp.arange(n_tokens)
    row = (t // GRID_W).astype(jnp.float32)
    col = (t % GRID_W).astype(jnp.float32)
    inv = ROPE_THETA ** (-jnp.arange(ROPE_PAIRS_PER_AXIS, dtype=jnp.float32) / ROPE_PAIRS_PER_AXIS)
    ang = jnp.concatenate([row[:, None] * inv[None, :], col[:, None] * inv[None, :]], axis=-1)
    return jnp.cos(ang), jnp.sin(ang)


def apply_rope(x, cos, sin):
    xf = x.astype(jnp.float32)
    x1, x2 = xf[..., 0::2], xf[..., 1::2]
    c, s = cos[None, :, None, :], sin[None, :, None, :]
    out = jnp.stack([x1 * c - x2 * s, x1 * s + x2 * c], axis=-1).reshape(x.shape)
    return out.astype(x.dtype)


def fourier_mix(u):
    b, s, _ = u.shape
    ug = u.astype(jnp.float32).reshape(b, s, N_FOURIER_GROUPS, FOURIER_GROUP_DIM)
    f = jnp.fft.fft2(ug, axes=(1, 3), norm="ortho").real
    return f.reshape(b, s, FOURIER_WIDTH).astype(u.dtype)


def attend_block(qb, k_all, v_all, sink, mask):
    scores = jnp.einsum('btkgd,blkd->bkgtl', qb, k_all,
                        preferred_element_type=jnp.float32) * (HEAD_DIM ** -0.5)
    if mask is not None:
        scores = jnp.where(mask[None, None, None], scores, NEG_INF)
    sink_col = jnp.broadcast_to(sink.astype(jnp.float32).reshape(N_KV_HEADS, Q_PER_KV)[None, :, :, None, None],
                                scores.shape[:-1] + (1,))
    p = jax.nn.softmax(jnp.concatenate([scores, sink_col], axis=-1), axis=-1)[..., :-1]
    return jnp.einsum('bkgtl,blkd->btkgd', p.astype(v_all.dtype), v_all)


def context_attention(q, k, v, sink):
    b, s = q.shape[:2]
    nb = s // BLOCK

    def one_block(i):
        qb = lax.dynamic_slice_in_dim(q, i * BLOCK, BLOCK, axis=1)
        return attend_block(qb, k, v, sink, None)

    out = lax.map(one_block, jnp.arange(nb))
    return jnp.moveaxis(out, 0, 1).reshape(b, s, ATTN_WIDTH)


def latent_attention(q, k, v, k_ctx, v_ctx, sink):
    b, s = q.shape[:2]
    nb = s // BLOCK
    k_pad = jnp.pad(k, ((0, 0), (BLOCK, BLOCK), (0, 0), (0, 0)))
    v_pad = jnp.pad(v, ((0, 0), (BLOCK, BLOCK), (0, 0), (0, 0)))
    q_off = jnp.arange(BLOCK)
    k_off = jnp.arange(3 * BLOCK) - BLOCK
    band = jnp.abs(k_off[None, :] - q_off[:, None]) <= WINDOW
    ctx_ok = jnp.ones((BLOCK, k_ctx.shape[1]), dtype=bool)

    def one_block(i):
        start = i * BLOCK
        qb = lax.dynamic_slice_in_dim(q, start, BLOCK, axis=1)
        kb = lax.dynamic_slice_in_dim(k_pad, start, 3 * BLOCK, axis=1)
        vb = lax.dynamic_slice_in_dim(v_pad, start, 3 * BLOCK, axis=1)
        kpos = start + k_off
        in_range = (kpos >= 0) & (kpos < s)
        mask = jnp.concatenate([band & in_range[None, :], ctx_ok], axis=-1)
        return attend_block(qb, jnp.concatenate([kb, k_ctx], axis=1),
                            jnp.concatenate([vb, v_ctx], axis=1), sink, mask)

    out = lax.map(one_block, jnp.arange(nb))
    return jnp.moveaxis(out, 0, 1).reshape(b, s, ATTN_WIDTH)


def split_projection(h, w_in):
    z = h @ w_in
    o1 = FOURIER_WIDTH
    o2 = o1 + ATTN_WIDTH
    o3 = o2 + KV_WIDTH
    o4 = o3 + KV_WIDTH
    return z[..., :o1], z[..., o1:o2], z[..., o2:o3], z[..., o3:o4], z[..., o4:]


def merge_branches(u_f, attn_o, g, w_fo, w_ao, w_out):
    gates = jax.nn.sigmoid(g)
    g_f, g_a = gates[..., :D_MODEL], gates[..., D_MODEL:]
    m = g_f * (fourier_mix(u_f) @ w_fo) + g_a * (attn_o @ w_ao)
    return m @ w_out


def channel_mixer(h, w_ff1, w_ff2):
    return jnp.square(jax.nn.relu(h @ w_ff1)) @ w_ff2


def context_layer(x, mod, g1, g2, w_in, sink, w_fo, w_ao, w_out, w_ff1, w_ff2):
    sh1, sc1, ga1, sh2, sc2, ga2 = mod
    b, s, _ = x.shape
    h = rmsnorm(x, g1) * (1 + sc1) + sh1
    u_f, q, k, v, g = split_projection(h, w_in)
    q = q.reshape(b, s, N_KV_HEADS, Q_PER_KV, HEAD_DIM)
    k = k.reshape(b, s, N_KV_HEADS, HEAD_DIM)
    v = v.reshape(b, s, N_KV_HEADS, HEAD_DIM)
    attn_o = context_attention(q, k, v, sink)
    x = x + ga1 * merge_branches(u_f, attn_o, g, w_fo, w_ao, w_out)
    h = rmsnorm(x, g2) * (1 + sc2) + sh2
    x = x + ga2 * channel_mixer(h, w_ff1, w_ff2)
    return x, k, v


def latent_layer(x, mod, k_ctx, v_ctx, cos, sin, g1, g2, w_in, sink, w_fo, w_ao, w_out, w_ff1, w_ff2):
    sh1, sc1, ga1, sh2, sc2, ga2 = mod
    b, s, _ = x.shape
    h = rmsnorm(x, g1) * (1 + sc1) + sh1
    u_f, q, k, v, g = split_projection(h, w_in)
    q = apply_rope(q.reshape(b, s, N_HEADS, HEAD_DIM), cos, sin).reshape(b, s, N_KV_HEADS, Q_PER_KV, HEAD_DIM)
    k = apply_rope(k.reshape(b, s, N_KV_HEADS, HEAD_DIM), cos, sin)
    v = v.reshape(b, s, N_KV_HEADS, HEAD_DIM)
    attn_o = latent_attention(q, k, v, k_ctx, v_ctx, sink)
    x = x + ga1 * merge_branches(u_f, attn_o, g, w_fo, w_ao, w_out)
    h = rmsnorm(x, g2) * (1 + sc2) + sh2
    x = x + ga2 * channel_mixer(h, w_ff1, w_ff2)
    return x


def setup_inputs(seed: int = 0) -> dict:
    key = jax.random.key(seed)
    ks = jax.random.split(key, 20)
    f32 = jnp.float32

    def w(k, shape, fan_in, scale=1.0):
        return jax.random.normal(k, shape, f32) * (scale * fan_in ** -0.5)

    return {
        "x_prompt": jax.random.normal(ks[0], (BATCH, SEQ, D_MODEL), f32),
        "x_sample": jax.random.normal(ks[1], (DEC_BATCH, DEC_SEQ, D_MODEL), f32),
        "c": jax.random.normal(ks[2], (DEC_BATCH, D_MODEL), f32),
        "cache_k": jax.random.normal(ks[3], (DEC_BATCH, DEPTH, PAST_LEN, N_KV_HEADS, HEAD_DIM), f32),
        "cache_v": jax.random.normal(ks[4], (DEC_BATCH, DEPTH, PAST_LEN, N_KV_HEADS, HEAD_DIM), f32),
        "c_ctx": jax.random.normal(ks[5], (D_MODEL,), f32),
        "w_ada": w(ks[6], (DEPTH, D_MODEL, 6 * D_MODEL), D_MODEL, 0.5),
        "b_ada": 0.01 * jax.random.normal(ks[7], (DEPTH, 6 * D_MODEL), f32),
        "norm1_g": 1.0 + 0.05 * jax.random.normal(ks[8], (DEPTH, D_MODEL), f32),
        "norm2_g": 1.0 + 0.05 * jax.random.normal(ks[9], (DEPTH, D_MODEL), f32),
        "w_in": w(ks[10], (DEPTH, D_MODEL, IN_WIDTH), D_MODEL),
        "sink": 0.5 * jax.random.normal(ks[11], (DEPTH, N_HEADS), f32),
        "w_fo": w(ks[12], (DEPTH, FOURIER_WIDTH, D_MODEL), FOURIER_WIDTH),
        "w_ao": w(ks[13], (DEPTH, ATTN_WIDTH, D_MODEL), ATTN_WIDTH),
        "w_out": w(ks[14], (DEPTH, D_MODEL, D_MODEL), D_MODEL),
        "w_ff1": w(ks[15], (DEPTH, D_MODEL, D_FF), D_MODEL),
        "w_ff2": w(ks[16], (DEPTH, D_FF, D_MODEL), D_FF),
        "final_g": 1.0 + 0.05 * jax.random.normal(ks[17], (D_MODEL,), f32),
    }


def reference(x_prompt, x_sample, c, cache_k, cache_v, c_ctx, w_ada, b_ada, norm1_g, norm2_g,
              w_in, sink, w_fo, w_ao, w_out, w_ff1, w_ff2, final_g):
    xc = x_prompt
    new_k, new_v = [], []
    for l in range(DEPTH):
        mod = modulation(c_ctx[None, :], w_ada[l], b_ada[l])
        xc, k_l, v_l = context_layer(xc, mod, norm1_g[l], norm2_g[l], w_in[l], sink[l],
                                     w_fo[l], w_ao[l], w_out[l], w_ff1[l], w_ff2[l])
        new_k.append(k_l)
        new_v.append(v_l)
    y_prompt = rmsnorm(xc, final_g)
    new_cache_k = jnp.stack(new_k, axis=1)
    new_cache_v = jnp.stack(new_v, axis=1)

    n_lat = x_sample.shape[1]
    rows = n_lat // GRID_W
    cos, sin = axial_rope_tables(rows, n_lat)
    xs = x_sample
    for l in range(DEPTH):
        mod = modulation(c, w_ada[l], b_ada[l])
        xs = latent_layer(xs, mod, cache_k[:, l], cache_v[:, l], cos, sin, norm1_g[l], norm2_g[l],
                          w_in[l], sink[l], w_fo[l], w_ao[l], w_out[l], w_ff1[l], w_ff2[l])
    y_sample = rmsnorm(xs, final_g)
    return (y_prompt, y_sample, new_cache_k, new_cache_v)
```

```python
import contextlib
import numpy as np
import ml_dtypes
import concourse.bass as bass
import concourse.mybir as mybir
from concourse.bass_utils import run_bass_kernel_spmd

F32 = mybir.dt.float32
BF16 = mybir.dt.bfloat16
AF = mybir.ActivationFunctionType
ALU = mybir.AluOpType
NPBF = ml_dtypes.bfloat16

DEPTH = 4
EPS = 1e-6
NSLOT = 4
STOP = None
P1STOP = None
ORDER = ['mod', 'p1', 'fourier', 'qproj', 'attn', 'merge', 'wo', 'mlp']


def _on(name):
    return STOP is None or ORDER.index(name) <= ORDER.index(STOP)


class Sched:
    ENG = ('pe', 'act', 'dve', 'pool', 'sp')

    def __init__(self, nc, sems, dma_sems):
        self.nc = nc
        self.q = {e: [] for e in self.ENG}
        self.sem = dict(sems)
        self.free_dma_sems = list(dma_sems)
        self.cnt = {}
        self.seen = {e: {} for e in self.ENG}
        self.last_w = {}
        self.readers = {}

    def _deps(self, engine, reads, writes):
        deps = {}

        def add(cv):
            if cv is None:
                return
            c, v = cv
            if c == engine and engine == 'pe':
                return
            if deps.get(c, 0) < v:
                deps[c] = v
        for r in reads:
            add(self.last_w.get(r))
        for w in writes:
            add(self.last_w.get(w))
            for c, v in self.readers.get(w, {}).items():
                add((c, v))
        return deps

    def _emit_waits(self, engine, deps):
        for c, v in deps.items():
            if c.startswith('dma:'):
                v = self.cnt[c]
            if self.seen[engine].get(c, 0) >= v:
                continue
            self.seen[engine][c] = v
            self.q[engine].append(('wait', self.sem[c], v))

    def _mark(self, counter, val, reads, writes):
        for r in reads:
            d = self.readers.setdefault(r, {})
            if d.get(counter, 0) < val:
                d[counter] = val
        for w in writes:
            self.last_w[w] = (counter, val)
            self.readers[w] = {}

    def op(self, engine, fns, reads=(), writes=()):
        if callable(fns):
            fns = [fns]
        deps = self._deps(engine, reads, writes)
        self._emit_waits(engine, deps)
        self.cnt[engine] = self.cnt.get(engine, 0) + 1
        val = self.cnt[engine]
        for f in fns[:-1]:
            self.q[engine].append(('ins', f, None))
        self.q[engine].append(('ins', fns[-1], (self.sem[engine], 1)))
        self._mark(engine, val, reads, writes)

    def dma(self, queue, semname, out, in_, reads=(), writes=()):
        c = 'dma:' + semname
        if c not in self.sem:
            self.sem[c] = self.free_dma_sems.pop()
        deps = self._deps(queue, reads, writes)
        self._emit_waits(queue, deps)
        self.cnt[c] = self.cnt.get(c, 0) + 16
        val = self.cnt[c]
        self.q[queue].append(('ins', (lambda e, out=out, in_=in_: e.dma_start(out=out, in_=in_)),
                              (self.sem[c], 16)))
        self._mark(c, val, reads, writes)

    def wait_all(self, engine, semname):
        c = 'dma:' + semname
        if c in self.cnt:
            self.q[engine].append(('wait', self.sem[c], self.cnt[c]))

    def replay(self, engine, e):
        for item in self.q[engine]:
            if item[0] == 'wait':
                e.wait_ge(item[1], item[2])
            else:
                ins = item[1](e)
                if item[2] is not None:
                    ins.then_inc(item[2][0], item[2][1])


CB_ONESM, CB_ONES1, CB_PSW, CB_C128, CB_NS128 = 0, 128, 256, 384, 512
CB_MPREV, CB_MNEXT, CB_CP, CB_SP, CB_N = 640, 1152, 1664, 2176, 2688
SF_BADA, SF_G1, SF_G2, SF_FG, SF_CV, SF_SINK, SF_N = 0, 192, 224, 256, 264, 280, 312


def _head_perm():
    hp = []
    for half in range(2):
        for j in range(4):
            hp += [half * 8 + j, half * 8 + 4 + j]
    return hp


def _const_tables():
    cb = np.zeros((128, CB_N), np.float32)
    cb[:, CB_ONESM:CB_ONESM + 128] = 1.0 / 1024.0
    cb[:, CB_ONES1:CB_ONES1 + 128] = 1.0
    p = np.arange(128)
    psw = np.zeros((128, 128), np.float32)
    psw[p, p ^ 1] = 1.0
    cb[:, CB_PSW:CB_PSW + 128] = psw
    ang = 2.0 * np.pi * np.outer(p, p) / 128.0
    cb[:, CB_C128:CB_C128 + 128] = np.cos(ang) / np.sqrt(128.0)
    cb[:, CB_NS128:CB_NS128 + 128] = -np.sin(ang) / np.sqrt(128.0)
    a = p[:, None]
    b = p[None, :]
    cb[:, CB_MPREV:CB_MPREV + 512] = np.tile((a >= b).astype(np.float32), (1, 4))
    cb[:, CB_MNEXT:CB_MNEXT + 512] = np.tile((a <= b).astype(np.float32), (1, 4))
    s = np.arange(256)
    ang = 2.0 * np.pi * np.outer(s, s) / 256.0
    c256 = (np.cos(ang) / 16.0).reshape(2, 128, 256).transpose(1, 0, 2).reshape(128, 512)
    s256 = (np.sin(ang) / 16.0).reshape(2, 128, 256).transpose(1, 0, 2).reshape(128, 512)
    cb[:, CB_CP:CB_CP + 512] = c256
    cb[:, CB_SP:CB_SP + 512] = s256
    s = np.arange(2048, dtype=np.float64)
    idx = (np.outer(s, s) % 2048)
    tab_c = np.cos(2.0 * np.pi * np.arange(2048) / 2048.0) / np.sqrt(2048.0)
    tab_s = np.sin(2.0 * np.pi * np.arange(2048) / 2048.0) / np.sqrt(2048.0)
    idx = idx.astype(np.int64)
    out = np.zeros((2, 8, 2, 128, 8, 256), np.float32)
    for k, tab in enumerate((tab_c, tab_s)):
        m = tab[idx].astype(np.float32)
        m = m.reshape(2, 8, 128, 8, 256)
        out[k] = m.transpose(3, 0, 2, 1, 4)
    dfts = out.reshape(2, 8, 2, 128, 2048).astype(NPBF)
    t = np.arange(2048)
    row = (t // 64).astype(np.float32)
    col = (t % 64).astype(np.float32)
    inv = (10000.0 ** (-np.arange(16, dtype=np.float32) / 16.0)).astype(np.float32)
    angs = np.concatenate([row[:, None] * inv[None, :], col[:, None] * inv[None, :]], axis=-1)
    cos = np.cos(angs).astype(np.float32)
    sin = np.sin(angs).astype(np.float32)
    rope = np.zeros((2, 128, 2048), np.float32)
    for pp in range(128):
        d = pp % 64
        pair = d // 2
        r = d % 2
        rope[0, pp] = cos[:, pair]
        rope[1, pp] = sin[:, pair] if r == 1 else -sin[:, pair]
    return cb.astype(NPBF), dfts, rope


def build(n_layers=DEPTH, do_prompt=True, do_sample=True, rope_dt=F32, mode=None):
    nc = bass.Bass("TRN2", target_bir_lowering=False)

    def din(name, shape, dt=F32):
        return nc.dram_tensor(name, shape, dt, kind="ExternalInput").ap()

    def dout(name, shape, dt=F32):
        return nc.dram_tensor(name, shape, dt, kind="ExternalOutput").ap()

    xpT = din("xpT", [1024, 1024])
    xsT = din("xsT", [1024, 2048])
    kcT = din("kcT", [DEPTH, 256, 256])
    vcd = din("vc", [DEPTH, 256, 256])
    W1 = din("W1", [DEPTH, 1024, 4096])
    wada = din("wada", [DEPTH, 1024, 6144])
    wfo = din("wfo", [DEPTH, 512, 1024])
    wao = din("wao", [DEPTH, 1024, 1024])
    wout = din("wout", [DEPTH, 1024, 1024])
    wff1 = din("wff1", [DEPTH, 1024, 4096])
    wff2 = din("wff2", [DEPTH, 4096, 1024])
    sfd = din("sf", [128, SF_N])
    cbd = din("cb", [128, CB_N], BF16)
    dftd = din("dfts", [2, 8, 2, 128, 2048], BF16)
    roped = din("rope", [2, 128, 2048])
    ypT = dout("ypT", [1024, 1024])
    ysT = dout("ysT", [1024, 2048])
    nkd = dout("nk", [4, DEPTH, 256, 256])
    nvd = dout("nv", [4, DEPTH, 256, 256])

    with contextlib.ExitStack() as es:
        def sb(name, shape, dt):
            return es.enter_context(nc.sbuf_tensor(name, shape, dt))

        def pst(name):
            return es.enter_context(nc.psum_tensor(name, [128, 512], F32))

        X = sb("X", [128, 8, 2048], F32)
        HT = sb("HT", [128, 8, 1024], BF16)
        B2 = sb("B2", [128, 16384], BF16)
        B4 = sb("B4", [128, 16384], BF16)
        Fb = sb("Fb", [128, 4, 1024], BF16)
        WS = [sb(f"WS{i}", [128, 2048], BF16) for i in range(NSLOT)]
        ROPE = sb("ROPE", [128, 2, 1024], rope_dt)
        PT = [sb(f"PT{i}", [128, 512], BF16) for i in range(3)]
        RSTD = sb("RSTD", [128, 512], F32)
        TMP32 = [sb(f"TMP{i}", [128, 512], F32) for i in range(3)]
        T16 = [sb(f"T16{i}", [128, 512], BF16) for i in range(4)]
        DC = [sb(f"DC{i}", [128, 256], BF16) for i in range(4)]
        DS = [sb(f"DS{i}", [128, 256], BF16) for i in range(2)]
        CB = sb("CB", [128, CB_N], BF16)
        SF = sb("SFs", [128, SF_N], F32)
        KC = sb("KC", [128, 2, 256], BF16)
        VC = sb("VC", [128, 2, 256], BF16)
        MODT = sb("MODT", [128, DEPTH, 48, 2], F32)
        GS = sb("GS", [128, DEPTH, 2, 8, 2], F32)
        ESb = sb("ES", [128, 32], F32)
        SCB = sb("SCB", [128, 8, 2], BF16)
        PSB = [pst(f"PS{i}") for i in range(8)]

        sems = {e: es.enter_context(nc.semaphore("s_" + e)) for e in Sched.ENG}
        dsems = [es.enter_context(nc.semaphore(f"d{i}")) for i in range(24)]
        block = es.enter_context(nc.Block())
        S = Sched(nc, sems, dsems)

        Qv = B2[:, 0:8192].rearrange("p (c t) -> p c t", c=8)
        ATTv = B2[:, 8192:16384].rearrange("p (c t) -> p c t", c=8)
        H2T = B2[:, :].rearrange("p (c t) -> p c t", c=8)
        Uv = B4[:, 0:8192].rearrange("p (s n) -> p s n", s=16)
        Vv = B4[:, 8192:12288].rearrange("p (s n) -> p s n", s=16)
        KTv = B4[:, 12288:16384].rearrange("p (c t) -> p c t", c=2)
        ATv = [B4[:, 0:8192].rearrange("p (c t) -> p c t", c=4),
               B4[:, 8192:16384].rearrange("p (c t) -> p c t", c=4)]

        def kX(c, tb): return ('X', c, tb)
        def kHT(c, lb): return ('HT', c, lb)
        def kQ(c, lb): return ('B2', c * 2 + lb)
        def kATT(c, lb): return ('B2', 16 + c * 2 + lb)
        def kH2(c, tb): return ('B2', c * 4 + tb)
        def kU(s): return ('B4', s)
        def kV(s): return ('B4', 16 + s // 2)
        def kKT(c, tb): return ('B4', 24 + c * 4 + tb)
        def kAT(g, kc, tb): return ('B4', g * 16 + kc * 4 + tb)
        def kF(g, lb): return ('F', g, lb)

        ONESM = CB[:, CB_ONESM:CB_ONESM + 128]
        ONES1 = CB[:, CB_ONES1:CB_ONES1 + 128]
        PSW = CB[:, CB_PSW:CB_PSW + 128]
        C128 = CB[:, CB_C128:CB_C128 + 128]
        NS128 = CB[:, CB_NS128:CB_NS128 + 128]
        MPREV = CB[:, CB_MPREV:CB_MPREV + 512]
        MNEXT = CB[:, CB_MNEXT:CB_MNEXT + 512]
        CPv = CB[:, CB_CP:CB_CP + 512].rearrange("p (s n) -> p s n", s=2)
        SPv = CB[:, CB_SP:CB_SP + 512].rearrange("p (s n) -> p s n", s=2)
        BADA = SF[:, SF_BADA:SF_BADA + 192].rearrange("p (l j) -> p l j", l=4)
        G12 = [SF[:, SF_G1:SF_G1 + 32].rearrange("p (l c) -> p l c", l=4),
               SF[:, SF_G2:SF_G2 + 32].rearrange("p (l c) -> p l c", l=4)]
        FG = SF[:, SF_FG:SF_FG + 8]
        CV = SF[:, SF_CV:SF_CV + 16].rearrange("p (c v) -> p c v", c=8)
        SINK = SF[:, SF_SINK:SF_SINK + 32]

        ctr = {'bank': 0, 'pt': 0, 'tmp': 0, 't16': 0, 'w': 0}
        NGEN = 4
        pending = []

        def defer(fn):
            pending.append(fn)

        def flush():
            while pending:
                pending.pop(0)()

        ctr['ngen'] = 7

        def bank():
            i = ctr['bank'] % ctr['ngen']
            ctr['bank'] = (i + 1) % ctr['ngen']
            return PSB[i], ('PS', i)

        def ptbuf():
            i = ctr['pt']
            ctr['pt'] = (i + 1) % len(PT)
            return PT[i], ('PT', i)

        def tmp32():
            i = ctr['tmp']
            ctr['tmp'] = (i + 1) % len(TMP32)
            return TMP32[i], ('TMP', i)

        def t16():
            i = ctr['t16']
            ctr['t16'] = (i + 1) % len(T16)
            return T16[i], ('T16', i)

        XS = [X[:, c, 1024:2048].bitcast(BF16) for c in range(8)]
        ctr['nslot'] = NSLOT

        def wload(src, kc, ncols):
            s = ctr['w'] % ctr['nslot']
            ctr['w'] = (s + 1) % ctr['nslot']
            buf = WS[s] if s < NSLOT else XS[s - NSLOT]
            view = buf[:, 0:kc * ncols].rearrange("p (k n) -> p k n", k=kc)
            S.dma('pool', f'w{s}', view, src, writes=[('W', s)])
            return view, ('W', s)

        def wsrc(w, l, r0, nrows, c0, ncols):
            return w[l, r0:r0 + nrows, c0:c0 + ncols].rearrange("(k p) n -> p k n", p=128)

        def chain(outap, okey, pairs, reads):
            n = len(pairs)
            fns = []
            for i, (lh, rh) in enumerate(pairs):
                fns.append(lambda e, lh=lh, rh=rh, i=i: e.matmul(outap, lh, rh, start=(i == 0), stop=(i == n - 1)))
            S.op('pe', fns, reads=reads, writes=[okey])


        def ACT(out, in_, func, reads, writes, scale=None, bias=None):
            kw = {}
            if scale is not None:
                kw['scale'] = scale
            if bias is not None:
                kw['bias'] = bias
            S.op('act', lambda e: e.activation(out=out, in_=in_, func=func, **kw), reads=reads, writes=writes)

        def TT(out, in0, in1, op, reads, writes):
            S.op('dve', lambda e: e.tensor_tensor(out=out, in0=in0, in1=in1, op=op), reads=reads, writes=writes)

        def STT(out, in0, scalar, in1, op0, op1, reads, writes):
            S.op('dve', lambda e: e.scalar_tensor_tensor(out=out, in0=in0, scalar=scalar, in1=in1, op0=op0, op1=op1), reads=reads, writes=writes)

        def TSADD(out, in0, scalar1, reads, writes):
            S.op('dve', lambda e: e.tensor_scalar(out=out, in0=in0, scalar1=scalar1, scalar2=None, op0=ALU.add), reads=reads, writes=writes)

        def COPY(out, in_, reads, writes):
            S.op('dve', lambda e: e.tensor_copy(out=out, in_=in_), reads=reads, writes=writes)

        def RECIP(out, in_, reads, writes):
            S.op('dve', lambda e: e.reciprocal(out=out, in_=in_), reads=reads, writes=writes)

        def MM(out, lh, rh, start, stop, reads, writes):
            S.op('pe', lambda e: e.matmul(out, lh, rh, start=start, stop=stop), reads=reads, writes=writes)

        S.dma('sp', 'c0', CB[:], cbd[:, :], writes=['CB'])
        S.dma('sp', 'c1', SF[:], sfd[:, :], writes=['SF'])
        S.op('act', lambda e: e.activation(out=SCB[:], in_=CV, func=AF.Silu), reads=['SF'], writes=['SCB'])
        S.op('act', lambda e: e.activation(out=ESb[:], in_=SINK, func=AF.Exp), reads=['SF'], writes=['ES'])

        def modulation_steps(l):
            bk = PSB[7]
            bkeys = [('DE', 1, 0), ('DE', 1, 1), ('PS', 7)]
            psv = bk[:, 0:96].rearrange("p (j v) -> p j v", v=2)
            steps = []

            held = {}

            def mk(js):
                def step():
                    if js > 0:
                        wv, wk = held.pop(js - 1)
                        for jj in range(2):
                            j = (js - 1) * 2 + jj
                            chain(bk[:, j * 2:j * 2 + 2], bkeys[0],
                                  [(wv[:, kc, jj * 128:(jj + 1) * 128], SCB[:, kc, :]) for kc in range(8)],
                                  [wk, 'SCB'] + (bkeys[1:] if j == 0 else []))
                    if js < 24:
                        held[js] = wload(wsrc(wada, l, 0, 1024, js * 256, 256), 8, 256)
                return step
            for js in range(25):
                steps.append(mk(js))

            def fin():
                for v in range(2):
                    TT(MODT[:, l, :, v], psv[:, :, v], BADA[:, l, :], ALU.add, bkeys + ['SF'], [('MOD', l)])
                for which in range(2):
                    off = 8 if which == 0 else 32
                    for v in range(2):
                        STT(GS[:, l, which, :, v], MODT[:, l, off:off + 8, v], 1.0, G12[which][:, l, :], ALU.add, ALU.mult,
                            [('MOD', l), 'SF'], [('GS', l)])
                S.last_w[bkeys[1]] = S.last_w[bkeys[0]]
                S.last_w[bkeys[2]] = S.last_w[bkeys[0]]
            steps.append(fin)
            return steps

        def mSH(l, which, c, v): return MODT[:, l, (0 if which == 0 else 24) + c, v:v + 1]
        def mGA(l, which, c, v): return MODT[:, l, (16 if which == 0 else 40) + c, v:v + 1]
        def mGS(l, which, c, v): return GS[:, l, which, c, v:v + 1]

        def rstd_block(t0, lb):
            tb = t0 // 512 + lb
            tok = slice(t0 + lb * 512, t0 + lb * 512 + 512)
            bk, bkk = bank()
            for c2 in range(4):
                sqs = []
                for c in (2 * c2, 2 * c2 + 1):
                    sq, sqk = t16()
                    if c % 2 == 0:
                        ACT(sq[:], X[:, c, tok], AF.Square, [kX(c, tb)], [sqk])
                    else:
                        TT(sq[:], X[:, c, tok], X[:, c, tok], ALU.mult, [kX(c, tb)], [sqk])
                    sqs.append((c, sq, sqk))
                for c, sq, sqk in sqs:
                    MM(bk[:], ONESM, sq[:], c == 0, c == 7, [sqk, 'CB'], [bkk])
            ACT(RSTD[:], bk[:], AF.Ln, [bkk], ['RSTD'], scale=1.0, bias=EPS)
            ACT(RSTD[:], RSTD[:], AF.Exp, ['RSTD'], ['RSTD'], scale=-0.5)
            return tb, tok

        def norm(l, which, v, t0, nblk, dst, dkey):
            for lb in range(nblk):
                tb, tok = rstd_block(t0, lb)
                for c in range(8):
                    tm, tmk = tmp32()
                    TT(tm[:], X[:, c, tok], RSTD[:], ALU.mult, [kX(c, tb), 'RSTD'], [tmk])
                    ACT(dst[:, c, lb * 512:(lb + 1) * 512], tm[:], AF.Identity, [tmk, ('MOD', l), ('GS', l)], [dkey(c, lb)],
                        scale=mGS(l, which, c, v), bias=mSH(l, which, c, v))

        def rope(bk, bkk, dst, dkeys, lb):
            zb, zbk = t16()
            ACT(zb[:], bk[:], AF.Identity, [bkk], [zbk])
            flush()

            def rest():
                b2, b2k = bank()
                MM(b2[:], PSW, zb[:], True, True, [zbk, 'CB'], [b2k])
                t1, t1k = tmp32()
                TT(t1[:], bk[:], ROPE[:, 0, lb * 512:(lb + 1) * 512], ALU.mult, [bkk, 'ROPE', zbk], [t1k])
                t2, t2k = tmp32()
                TT(t2[:], b2[:], ROPE[:, 1, lb * 512:(lb + 1) * 512], ALU.mult, [b2k, 'ROPE'], [t2k])
                TT(dst, t1[:], t2[:], ALU.add, [t1k, t2k], dkeys)
            defer(rest)

        def phase1(l, v, t0, sample):
            norm(l, 0, v, t0, 2, HT, kHT)
            if sample:
                for k in range(2):
                    S.dma('sp', 'rope', ROPE[:, k, :], roped[k, :, t0:t0 + 1024], writes=['ROPE'])
            if P1STOP == 'norm':
                return
            for uh in range(2):
                wv, wk = wload(wsrc(W1, l, 0, 1024, uh * 256, 256), 8, 256)
                for tt in range(8):
                    gt = t0 // 128 + tt
                    bk, bkk = bank()
                    chain(bk[:, 0:256], bkk, [(HT[:, kc, tt * 128:(tt + 1) * 128], wv[:, kc, :]) for kc in range(8)],
                          [wk] + [kHT(c, tt // 4) for c in range(8)])
                    ACT(Uv[:, gt, uh * 256:(uh + 1) * 256], bk[:, 0:256], AF.Identity, [bkk], [kU(gt)])
            if P1STOP == 'U':
                return
            wvv, wvk = wload(wsrc(W1, l, 0, 1024, 512, 256), 8, 256)
            wkv, wkk = wload(wsrc(W1, l, 0, 1024, 768, 256), 8, 256)
            for tt in range(8):
                gt = t0 // 128 + tt
                hk = [kHT(c, tt // 4) for c in range(8)]
                bk, bkk = bank()
                chain(bk[:, 0:256], bkk, [(HT[:, kc, tt * 128:(tt + 1) * 128], wvv[:, kc, :]) for kc in range(8)], [wvk] + hk)
                if sample:
                    COPY(Vv[:, gt, :], bk[:, 0:256], [bkk], [kV(gt)])
                else:
                    b2, b2k = bank()
                    chain(b2[:, 0:256], b2k, [(HT[:, kc, tt * 128:(tt + 1) * 128], wkv[:, kc, :]) for kc in range(8)], [wkk] + hk)
                    st, stk = tmp32()
                    ACT(st[:, 0:256], bk[:, 0:256], AF.Identity, [bkk], [stk])
                    ACT(st[:, 256:512], b2[:, 0:256], AF.Identity, [b2k], [stk])
                    COPY(Vv[:, gt, :], st[:, 0:256], [stk], [kV(gt)])
                    b = tt // 2
                    s0 = (tt % 2) * 128
                    if P1STOP not in ('Vnodma', 'Vk'):
                        S.dma('sp', 'out', nvd[b, l, s0:s0 + 128, :], st[:, 0:256], reads=[stk], writes=[('OUT', 'nv', l, tt)])
                        S.dma('sp', 'out', nkd[b, l, s0:s0 + 128, :], st[:, 256:512], reads=[stk], writes=[('OUT', 'nk', l, tt)])
            if P1STOP in ('V', 'Vnodma', 'Vonly', 'Vk'):
                return
            for c in range(2):
                for lb in range(2):
                    tb = t0 // 512 + lb
                    bk, bkk = bank()
                    chain(bk[:], bkk, [(wkv[:, kc, c * 128:(c + 1) * 128], HT[:, kc, lb * 512:(lb + 1) * 512]) for kc in range(8)],
                          [wkk] + [kHT(cc, lb) for cc in range(8)])
                    dst = KTv[:, c, t0 + lb * 512:t0 + lb * 512 + 512]
                    if sample:
                        rope(bk, bkk, dst, [kKT(c, tb)], lb)
                    else:
                        COPY(dst, bk[:], [bkk], [kKT(c, tb)])
            flush()

        def fourier(l, t0, sample, hook=None):
            for qi in range(4):
                if qi == 1 and hook is not None:
                    hook()
                if sample:
                    qu = t0 // 256 + qi
                    slots = []
                    for k in range(2):
                        for sh in range(2):
                            slots.append(wload(dftd[k, qu, sh].rearrange("p (s n) -> p s n", s=8), 8, 256))
                    nst = 16
                    rcl = [(slots[st // 8][0][:, st % 8, :], slots[st // 8][1]) for st in range(16)]
                    rsl = [(slots[2 + st // 8][0][:, st % 8, :], slots[2 + st // 8][1]) for st in range(16)]
                    ut0 = 0
                else:
                    nst = 2
                    rcl = [(CPv[:, st, :], 'CB') for st in range(2)]
                    rsl = [(SPv[:, st, :], 'CB') for st in range(2)]
                    ut0 = qi * 2
                for g in range(4):
                    bk, bkk = bank()
                    prs = [(Uv[:, ut0 + st, g * 128:(g + 1) * 128], rcl[st][0]) for st in range(nst)]
                    rk = list({rcl[st][1] for st in range(nst)}) + [kU(ut0 + st) for st in range(nst)]
                    chain(bk[:, 0:256], bkk, prs, rk)
                    COPY(DC[g][:], bk[:, 0:256], [bkk], [('DC', g)])
                    if g == 0:
                        flush()
                for g in range(4):
                    bk, bkk = bank()
                    prs = [(Uv[:, ut0 + st, g * 128:(g + 1) * 128], rsl[st][0]) for st in range(nst)]
                    rk = list({rsl[st][1] for st in range(nst)}) + [kU(ut0 + st) for st in range(nst)]
                    chain(bk[:, 0:256], bkk, prs, rk)
                    COPY(DS[g % 2][:], bk[:, 0:256], [bkk], [('DS', g % 2)])
                    flush()

                    def rest(g=g, qi=qi):
                        b2, b2k = bank()
                        chain(b2[:, 0:256], b2k, [(C128, DC[g][:]), (NS128, DS[g % 2][:])], ['CB', ('DC', g), ('DS', g % 2)])
                        ACT(Fb[:, g, qi * 256:(qi + 1) * 256], b2[:, 0:256], AF.Identity, [b2k], [('Fq', g, qi)])
                    defer(rest)
            flush()

        def kFr(g, lb): return [('Fq', g, lb * 2), ('Fq', g, lb * 2 + 1)]

        def qnorm(l, v, t0):
            for k in range(2):
                S.dma('sp', 'rope', ROPE[:, k, :], roped[k, :, t0:t0 + 1024], writes=['ROPE'])
            norm(l, 0, v, t0, 2, HT, kHT)

        def qproj(l, v, t0, sample):
            for cs in range(4):
                wv, wk = wload(wsrc(W1, l, 0, 1024, 1024 + cs * 256, 256), 8, 256)
                for cc in range(2):
                    c = cs * 2 + cc
                    for lb in range(2):
                        bk, bkk = bank()
                        chain(bk[:], bkk, [(wv[:, kc, cc * 128:(cc + 1) * 128], HT[:, kc, lb * 512:(lb + 1) * 512]) for kc in range(8)],
                              [wk] + [kHT(k2, lb) for k2 in range(8)])
                        dst = Qv[:, c, lb * 512:(lb + 1) * 512]
                        if sample:
                            rope(bk, bkk, dst, [kQ(c, lb)], lb)
                        else:
                            ACT(dst, bk[:], AF.Identity, [bkk], [kQ(c, lb)])
            flush()

        actr = {'n': 0}
        att_pend = []

        def attention(l, t0, sample):
            ctr['ngen'] = 4
            for i in range(8):
                gi = t0 // 128 + i
                if sample:
                    keys = []
                    if gi > 0:
                        keys.append((0, gi - 1, MPREV))
                    keys.append((0, gi, None))
                    if gi < 15:
                        keys.append((0, gi + 1, MNEXT))
                    keys += [(1, 0, None), (1, 1, None)]
                else:
                    sq = i // 2
                    keys = [(0, 2 * sq, None), (0, 2 * sq + 1, None)]
                nk = len(keys)
                lb = i // 4
                for gp in range(2):
                    par = actr['n'] % 2
                    actr['n'] += 1
                    OBs = [PSB[4 + 2 * par], PSB[4 + 2 * par]]
                    DENs = [PSB[5 + 2 * par], PSB[5 + 2 * par]]
                    qk = {}

                    def emit_qk(idx):
                        src, kt, msk = keys[idx]
                        for gg in range(2):
                            pb = gg * 64
                            bk, bkk = bank()
                            if src == 0:
                                lh = KTv[pb:pb + 64, gp, kt * 128:(kt + 1) * 128]
                                lk = kKT(gp, kt // 4)
                            else:
                                lh = KC[pb:pb + 64, gp, kt * 128:(kt + 1) * 128]
                                lk = 'KC'
                            rh = Qv[pb:pb + 64, gp * 4:(gp + 1) * 4, i * 128:(i + 1) * 128]
                            MM(bk[:], lh, rh, True, True, [lk] + [kQ(gp * 4 + j, lb) for j in range(4)], [bkk])
                            qk[(idx, gg)] = (bk, bkk)

                    emit_qk(0)
                    if nk > 1:
                        emit_qk(1)
                    for idx in range(nk):
                        src, kt, msk = keys[idx]
                        pts = []
                        for gg in range(2):
                            bk, bkk = qk[(idx, gg)]
                            pt, ptk = ptbuf()
                            ACT(pt[:], bk[:], AF.Exp, [bkk], [ptk], scale=0.125)
                            if msk is not None:
                                TT(pt[:], pt[:], msk, ALU.mult, [ptk, 'CB'], [ptk])
                            pts.append((pt, ptk))
                        if idx == min(1, nk - 1):
                            while att_pend:
                                att_pend.pop(0)()
                        for gg in range(2):
                            g = gp * 2 + gg
                            pb = gg * 64
                            pt, ptk = pts[gg]
                            if src == 0:
                                vv = Vv[:, kt, g * 64:(g + 1) * 64]
                                vk = kV(kt)
                            else:
                                vv = VC[:, kt, g * 64:(g + 1) * 64]
                                vk = 'VC'
                            MM(OBs[gg][pb:pb + 64, :], vv, pt[:], idx == 0, idx == nk - 1, [vk, ptk], [('OB', par, gg), ('PS', 4 + 2 * par)])
                        for gg in range(2):
                            pb = gg * 64
                            pt, ptk = pts[gg]
                            MM(DENs[gg][pb:pb + 64, :], ONES1[:, 0:64], pt[:], idx == 0, idx == nk - 1, [ptk, 'CB'], [('DE', par, gg), ('PS', 5 + 2 * par)])
                        if idx + 2 < nk:
                            emit_qk(idx + 2)
                    def norm_att(OB=OBs[0], DEN=DENs[0], par=par, gp=gp, i=i, lb=lb):
                        ds_, dsk = tmp32()
                        for j in range(4):
                            TSADD(ds_[:, j * 128:(j + 1) * 128], DEN[:, j * 128:(j + 1) * 128],
                                  ESb[:, l * 8 + gp * 4 + j:l * 8 + gp * 4 + j + 1], [('DE', par, 0), ('DE', par, 1), ('PS', 5 + 2 * par), 'ES'], [dsk])
                        ACT(ds_[:], ds_[:], AF.Ln, [dsk], [dsk])
                        ACT(ds_[:], ds_[:], AF.Exp, [dsk], [dsk], scale=-1.0)
                        TT(ATTv[:, gp * 4:(gp + 1) * 4, i * 128:(i + 1) * 128],
                           OB[:, :].rearrange("p (j q) -> p j q", j=4),
                           ds_[:, :].rearrange("p (j q) -> p j q", j=4), ALU.mult,
                           [('OB', par, 0), ('OB', par, 1), ('PS', 4 + 2 * par), dsk],
                           [('ATTp', gp * 4 + j, lb, i) for j in range(4)] + [kATT(gp * 4 + j, lb) for j in range(4)])
                    att_pend.append(norm_att)
            while att_pend:
                att_pend.pop(0)()
            ctr['ngen'] = 7

        def kATTr(c, lb):
            return [('ATTp', c, lb, i) for i in range(lb * 4, lb * 4 + 4)] + [kATT(c, lb)]

        def merge(l, v, t0):
            for cp in range(4):
                fo, fok = wload(wsrc(wfo, l, 0, 512, cp * 256, 256), 4, 256)
                ao, aok = wload(wsrc(wao, l, 0, 1024, cp * 256, 256), 8, 256)
                gf, gfk = wload(wsrc(W1, l, 0, 1024, 2048 + cp * 256, 256), 8, 256)
                ga, gak = wload(wsrc(W1, l, 0, 1024, 3072 + cp * 256, 256), 8, 256)
                for cc in range(2):
                    c = cp * 2 + cc
                    cs = slice(cc * 128, (cc + 1) * 128)
                    for lb in range(2):
                        ts = slice(lb * 512, (lb + 1) * 512)
                        bfo, bfok = bank()
                        chain(bfo[:], bfok, [(fo[:, kc, cs], Fb[:, kc, ts]) for kc in range(4)],
                              [fok] + [k for g in range(4) for k in kFr(g, lb)])
                        bgf, bgfk = bank()
                        chain(bgf[:], bgfk, [(gf[:, kc, cs], HT[:, kc, ts]) for kc in range(8)], [gfk] + [kHT(k2, lb) for k2 in range(8)])
                        sf, sfk = tmp32()
                        ACT(sf[:], bgf[:], AF.Sigmoid, [bgfk], [sfk])
                        TT(sf[:], bfo[:], sf[:], ALU.mult, [bfok, sfk], [sfk])
                        bao, baok = bank()
                        chain(bao[:], baok, [(ao[:, kc, cs], ATTv[:, kc, ts]) for kc in range(8)],
                              [aok] + [k for k2 in range(8) for k in kATTr(k2, lb)])
                        bga, bgak = bank()
                        chain(bga[:], bgak, [(ga[:, kc, cs], HT[:, kc, ts]) for kc in range(8)], [gak] + [kHT(k2, lb) for k2 in range(8)])
                        sa, sak = tmp32()
                        ACT(sa[:], bga[:], AF.Sigmoid, [bgak], [sak])
                        TT(sa[:], bao[:], sa[:], ALU.mult, [baok, sak], [sak])
                        TT(Qv[:, c, ts], sf[:], sa[:], ALU.add, [sfk, sak], [kQ(c, lb)])

        def wo(l, v, t0):
            for cp in range(4):
                wv, wk = wload(wsrc(wout, l, 0, 1024, cp * 256, 256), 8, 256)
                for cc in range(2):
                    c = cp * 2 + cc
                    for lb in range(2):
                        tb = t0 // 512 + lb
                        tok = slice(t0 + lb * 512, t0 + lb * 512 + 512)
                        bk, bkk = bank()
                        chain(bk[:], bkk, [(wv[:, kc, cc * 128:(cc + 1) * 128], Qv[:, kc, lb * 512:(lb + 1) * 512]) for kc in range(8)],
                              [wk] + [kQ(k2, lb) for k2 in range(8)])
                        STT(X[:, c, tok], bk[:], mGA(l, 0, c, v), X[:, c, tok], ALU.mult, ALU.add,
                            [bkk, kX(c, tb), ('MOD', l)], [kX(c, tb)])

        def mlp(l, v, nblk, hook=None):
            norm(l, 1, v, 0, nblk, H2T, kH2)
            for hg in range(8):
                ab = hg % 2
                for sh in range(2):
                    if hook is not None:
                        hook()
                    wv, wk = wload(wsrc(wff1, l, 0, 1024, hg * 512 + sh * 256, 256), 8, 256)
                    for cc in range(2):
                        kcl = sh * 2 + cc
                        for lb in range(nblk):
                            ts = slice(lb * 512, (lb + 1) * 512)
                            bk, bkk = bank()
                            chain(bk[:], bkk, [(wv[:, kc, cc * 128:(cc + 1) * 128], H2T[:, kc, ts]) for kc in range(8)],
                                  [wk] + [kH2(k2, lb) for k2 in range(8)])
                            tm, tmk = tmp32()
                            ACT(tm[:], bk[:], AF.Relu, [bkk], [tmk])
                            TT(ATv[ab][:, kcl, ts], tm[:], tm[:], ALU.mult, [tmk], [kAT(ab, kcl, lb)])
                for oh in range(2):
                    if hook is not None:
                        hook()
                    wv, wk = wload(wsrc(wff2, l, hg * 512, 512, oh * 512, 512), 4, 512)
                    for cc in range(4):
                        c = oh * 4 + cc
                        for lb in range(nblk):
                            ts = slice(lb * 512, (lb + 1) * 512)
                            bk, bkk = bank()
                            chain(bk[:], bkk, [(wv[:, kc, cc * 128:(cc + 1) * 128], ATv[ab][:, kc, ts]) for kc in range(4)],
                                  [wk] + [kAT(ab, k2, lb) for k2 in range(4)])
                            STT(X[:, c, ts], bk[:], mGA(l, 1, c, v), X[:, c, ts], ALU.mult, ALU.add,
                                [bkk, kX(c, lb), ('MOD', l)], [kX(c, lb)])

        def final(nblk, outd):
            for lb in range(nblk):
                tb, tok = rstd_block(0, lb)
                for c in range(8):
                    tm, tmk = tmp32()
                    TT(tm[:], X[:, c, tok], RSTD[:], ALU.mult, [kX(c, tb), 'RSTD'], [tmk])
                    ACT(tm[:], tm[:], AF.Identity, [tmk, 'SF'], [tmk], scale=FG[:, c:c + 1])
                    S.dma('sp', 'out', outd[c * 128:(c + 1) * 128, tok], tm[:], reads=[tmk], writes=[('OUT', 'y', c, lb)])

        def load_x(xd, T):
            for c in range(8):
                extra = [('W', NSLOT + c)] if T > 1024 else []
                S.dma('sp', 'x', X[:, c, 0:T], xd[c * 128:(c + 1) * 128, :], writes=[kX(c, tb) for tb in range(T // 512)] + extra)

        if do_prompt:
            load_x(xpT, 1024)
            ctr['nslot'] = NSLOT + 8
        for l in range(n_layers):
            if l == 0 or not do_prompt:
                for st in modulation_steps(l):
                    st()
            nxt = modulation_steps(l + 1) if (do_prompt and l + 1 < n_layers) else []

            def mhook(nxt=nxt):
                if len(nxt) > 1:
                    nxt.pop(0)()
            if do_prompt:
                if _on('p1'): phase1(l, 0, 0, False)
                if _on('fourier'): fourier(l, 0, False)
                if _on('qproj'): qproj(l, 0, 0, False)
                if _on('attn'): attention(l, 0, False)
                if _on('merge'): merge(l, 0, 0)
                if _on('wo'): wo(l, 0, 0)
                if _on('mlp'): mlp(l, 0, 2, hook=mhook)
            while nxt:
                nxt.pop(0)()
        if mode == 'io':
          for rep in range(10):
            for c in range(8):
                S.dma('sp', 'out', ypT[c * 128:(c + 1) * 128, :], X[:, c, 0:1024], reads=[kX(c, 0), kX(c, 1)], writes=[('OUT', c, rep)])
        elif mode == 'sq':
            for c in range(8):
                tm, tmk = tmp32()
                ACT(tm[:], X[:, c, 0:512], AF.Square, [kX(c, 0)], [tmk])
                S.dma('sp', 'out', ypT[c * 128:(c + 1) * 128, 0:512], tm[:], reads=[tmk], writes=[('OUT', c)])
        elif do_prompt:
            final(2, ypT)
        ctr['nslot'] = NSLOT
        ctr['w'] = 0
        if do_sample:
            load_x(xsT, 2048)
            for l in range(n_layers):
                S.dma('pool', 'kc', KC[:], kcT[l].rearrange("(c p) t -> p c t", p=128), writes=['KC'])
                S.dma('pool', 'vc', VC[:], vcd[l].rearrange("(s p) n -> p s n", p=128), writes=['VC'])
                for hf in range(2):
                    if _on('p1'): phase1(l, 1, hf * 1024, True)
                for hf in range(2):
                    t0 = hf * 1024
                    if _on('fourier'): fourier(l, t0, True, hook=(lambda l=l, t0=t0: qnorm(l, 1, t0)))
                    if _on('qproj'): qproj(l, 1, t0, True)
                    if _on('attn'): attention(l, t0, True)
                    if _on('merge'): merge(l, 1, t0)
                    if _on('wo'): wo(l, 1, t0)
                if _on('mlp'): mlp(l, 1, 4)
            final(4, ysT)
        S.wait_all('sp', 'out')

        @block.sync
        def _(e):
            S.replay('sp', e)

        @block.scalar
        def _(e):
            S.replay('act', e)

        @block.vector
        def _(e):
            S.replay('dve', e)

        @block.gpsimd
        def _(e):
            S.replay('pool', e)

        @block.tensor
        def _(e):
            S.replay('pe', e)
    return nc, S


_CONST = None


def _prep(inputs):
    global _CONST
    if _CONST is None:
        _CONST = _const_tables()
    cb, dfts, rope = _CONST
    f = lambda a: np.ascontiguousarray(np.asarray(a, dtype=np.float32))
    w_in = f(inputs['w_in'])
    hp = _head_perm()
    qcols = np.concatenate([np.arange(512 + h * 64, 512 + (h + 1) * 64) for h in hp])
    W1 = np.ascontiguousarray(np.concatenate(
        [w_in[:, :, 0:512], w_in[:, :, 1792:2048], w_in[:, :, 1536:1792], w_in[:, :, qcols], w_in[:, :, 2048:4096]], axis=2))
    wao = np.ascontiguousarray(f(inputs['w_ao'])[:, qcols - 512, :])
    shared = dict(W1=W1, wada=f(inputs['w_ada']), wfo=f(inputs['w_fo']), wao=wao, wout=f(inputs['w_out']),
                  wff1=f(inputs['w_ff1']), wff2=f(inputs['w_ff2']), cb=cb, dfts=dfts, rope=rope)
    x_prompt = f(inputs['x_prompt'])
    x_sample = f(inputs['x_sample'])
    c = f(inputs['c'])
    c_ctx = f(inputs['c_ctx'])
    ck = f(inputs['cache_k'])
    cv = f(inputs['cache_v'])
    b_ada = f(inputs['b_ada'])
    sfbase = np.zeros((128, SF_N), np.float32)
    sfbase[:, SF_BADA:SF_BADA + 192] = b_ada.reshape(4, 48, 128).transpose(2, 0, 1).reshape(128, 192)
    sfbase[:, SF_G1:SF_G1 + 32] = f(inputs['norm1_g']).reshape(4, 8, 128).transpose(2, 0, 1).reshape(128, 32)
    sfbase[:, SF_G2:SF_G2 + 32] = f(inputs['norm2_g']).reshape(4, 8, 128).transpose(2, 0, 1).reshape(128, 32)
    sfbase[:, SF_FG:SF_FG + 8] = f(inputs['final_g']).reshape(8, 128).T
    snk = f(inputs['sink']).reshape(4, 2, 2, 4)
    sfbase[0:64, SF_SINK:SF_SINK + 32] = np.broadcast_to(snk[:, :, 0, :].reshape(1, 32), (64, 32))
    sfbase[64:128, SF_SINK:SF_SINK + 32] = np.broadcast_to(snk[:, :, 1, :].reshape(1, 32), (64, 32))
    in_maps = []
    for i in range(8):
        m = dict(shared)
        m['xpT'] = np.ascontiguousarray(x_prompt[4 * i:4 * i + 4].reshape(1024, 1024).T)
        m['xsT'] = np.ascontiguousarray(x_sample[i].T)
        m['kcT'] = np.ascontiguousarray(ck[i].reshape(4, 256, 256).transpose(0, 2, 1))
        m['vc'] = np.ascontiguousarray(cv[i].reshape(4, 256, 256))
        sfa = sfbase.copy()
        cvv = np.stack([c_ctx, c[i]], axis=0)
        sfa[:, SF_CV:SF_CV + 16] = cvv.reshape(2, 8, 128).transpose(2, 1, 0).reshape(128, 16)
        m['sf'] = sfa
        in_maps.append(m)
    return in_maps


_NC = None


def kernel(**inputs):
    global _NC
    in_maps = _prep(inputs)
    if _NC is None:
        _NC = build()[0]
    res = run_bass_kernel_spmd(_NC, in_maps, core_ids=list(range(8)))
    y_prompt = np.zeros((32, 256, 1024), np.float32)
    y_sample = np.zeros((8, 2048, 1024), np.float32)
    nk = np.zeros((32, 4, 256, 4, 64), np.float32)
    nv = np.zeros((32, 4, 256, 4, 64), np.float32)
    for i in range(8):
        r = res.results[i]
        y_prompt[4 * i:4 * i + 4] = r['ypT'].T.reshape(4, 256, 1024)
        y_sample[i] = r['ysT'].T
        nk[4 * i:4 * i + 4] = r['nk'].reshape(4, 4, 256, 4, 64)
        nv[4 * i:4 * i + 4] = r['nv'].reshape(4, 4, 256, 4, 64)
    return (y_prompt, y_sample, nk, nv)
```

```python
import contextlib
import numpy as np
import ml_dtypes
import concourse.bass as bass
import concourse.mybir as mybir
from concourse.bass_utils import run_bass_kernel_spmd

F32 = mybir.dt.float32
BF16 = mybir.dt.bfloat16
AF = mybir.ActivationFunctionType
ALU = mybir.AluOpType
NPBF = ml_dtypes.bfloat16

DEPTH = 4
EPS = 1e-6
NSLOT = 4
STOP = None
P1STOP = None
ORDER = ['mod', 'p1', 'fourier', 'qproj', 'attn', 'merge', 'wo', 'mlp']


def _on(name):
    return STOP is None or ORDER.index(name) <= ORDER.index(STOP)


class Sched:
    ENG = ('pe', 'act', 'dve', 'pool', 'sp')

    def __init__(self, nc, sems, dma_sems):
        self.nc = nc
        self.q = {e: [] for e in self.ENG}
        self.sem = dict(sems)
        self.free_dma_sems = list(dma_sems)
        self.cnt = {}
        self.seen = {e: {} for e in self.ENG}
        self.last_w = {}
        self.readers = {}

    def _deps(self, engine, reads, writes):
        deps = {}

        def add(cv):
            if cv is None:
                return
            c, v = cv
            if c == engine and engine == 'pe':
                return
            if deps.get(c, 0) < v:
                deps[c] = v
        for r in reads:
            add(self.last_w.get(r))
        for w in writes:
            add(self.last_w.get(w))
            for c, v in self.readers.get(w, {}).items():
                add((c, v))
        return deps

    def _emit_waits(self, engine, deps):
        for c, v in deps.items():
            if c.startswith('dma:'):
                v = self.cnt[c]
            if self.seen[engine].get(c, 0) >= v:
                continue
            self.seen[engine][c] = v
            self.q[engine].append(('wait', self.sem[c], v))

    def _mark(self, counter, val, reads, writes):
        for r in reads:
            d = self.readers.setdefault(r, {})
            if d.get(counter, 0) < val:
                d[counter] = val
        for w in writes:
            self.last_w[w] = (counter, val)
            self.readers[w] = {}

    def op(self, engine, fns, reads=(), writes=()):
        if callable(fns):
            fns = [fns]
        deps = self._deps(engine, reads, writes)
        self._emit_waits(engine, deps)
        self.cnt[engine] = self.cnt.get(engine, 0) + 1
        val = self.cnt[engine]
        for f in fns[:-1]:
            self.q[engine].append(('ins', f, None))
        self.q[engine].append(('ins', fns[-1], (self.sem[engine], 1)))
        self._mark(engine, val, reads, writes)

    def dma(self, queue, semname, out, in_, reads=(), writes=()):
        c = 'dma:' + semname
        if c not in self.sem:
            self.sem[c] = self.free_dma_sems.pop()
        deps = self._deps(queue, reads, writes)
        self._emit_waits(queue, deps)
        self.cnt[c] = self.cnt.get(c, 0) + 16
        val = self.cnt[c]
        self.q[queue].append(('ins', (lambda e, out=out, in_=in_: e.dma_start(out=out, in_=in_)),
                              (self.sem[c], 16)))
        self._mark(c, val, reads, writes)

    def wait_all(self, engine, semname):
        c = 'dma:' + semname
        if c in self.cnt:
            self.q[engine].append(('wait', self.sem[c], self.cnt[c]))

    def replay(self, engine, e):
        for item in self.q[engine]:
            if item[0] == 'wait':
                e.wait_ge(item[1], item[2])
            else:
                ins = item[1](e)
                if item[2] is not None:
                    ins.then_inc(item[2][0], item[2][1])


CB_ONESM, CB_ONES1, CB_PSW, CB_C128, CB_NS128 = 0, 128, 256, 384, 512
CB_MPREV, CB_MNEXT, CB_CP, CB_SP, CB_N = 640, 1152, 1664, 2176, 2688
SF_BADA, SF_G1, SF_G2, SF_FG, SF_CV, SF_SINK, SF_N = 0, 192, 224, 256, 264, 280, 312


def _head_perm():
    hp = []
    for half in range(2):
        for j in range(4):
            hp += [half * 8 + j, half * 8 + 4 + j]
    return hp


def _const_tables():
    cb = np.zeros((128, CB_N), np.float32)
    cb[:, CB_ONESM:CB_ONESM + 128] = 1.0 / 1024.0
    cb[:, CB_ONES1:CB_ONES1 + 128] = 1.0
    p = np.arange(128)
    psw = np.zeros((128, 128), np.float32)
    psw[p, p ^ 1] = 1.0
    cb[:, CB_PSW:CB_PSW + 128] = psw
    ang = 2.0 * np.pi * np.outer(p, p) / 128.0
    cb[:, CB_C128:CB_C128 + 128] = np.cos(ang) / np.sqrt(128.0)
    cb[:, CB_NS128:CB_NS128 + 128] = -np.sin(ang) / np.sqrt(128.0)
    a = p[:, None]
    b = p[None, :]
    cb[:, CB_MPREV:CB_MPREV + 512] = np.tile((a >= b).astype(np.float32), (1, 4))
    cb[:, CB_MNEXT:CB_MNEXT + 512] = np.tile((a <= b).astype(np.float32), (1, 4))
    s = np.arange(256)
    ang = 2.0 * np.pi * np.outer(s, s) / 256.0
    c256 = (np.cos(ang) / 16.0).reshape(2, 128, 256).transpose(1, 0, 2).reshape(128, 512)
    s256 = (np.sin(ang) / 16.0).reshape(2, 128, 256).transpose(1, 0, 2).reshape(128, 512)
    cb[:, CB_CP:CB_CP + 512] = c256
    cb[:, CB_SP:CB_SP + 512] = s256
    s = np.arange(2048, dtype=np.float64)
    idx = (np.outer(s, s) % 2048)
    tab_c = np.cos(2.0 * np.pi * np.arange(2048) / 2048.0) / np.sqrt(2048.0)
    tab_s = np.sin(2.0 * np.pi * np.arange(2048) / 2048.0) / np.sqrt(2048.0)
    idx = idx.astype(np.int64)
    out = np.zeros((2, 8, 2, 128, 8, 256), np.float32)
    for k, tab in enumerate((tab_c, tab_s)):
        m = tab[idx].astype(np.float32)
        m = m.reshape(2, 8, 128, 8, 256)
        out[k] = m.transpose(3, 0, 2, 1, 4)
    dfts = out.reshape(2, 8, 2, 128, 2048).astype(NPBF)
    t = np.arange(2048)
    row = (t // 64).astype(np.float32)
    col = (t % 64).astype(np.float32)
    inv = (10000.0 ** (-np.arange(16, dtype=np.float32) / 16.0)).astype(np.float32)
    angs = np.concatenate([row[:, None] * inv[None, :], col[:, None] * inv[None, :]], axis=-1)
    cos = np.cos(angs).astype(np.float32)
    sin = np.sin(angs).astype(np.float32)
    rope = np.zeros((2, 128, 2048), np.float32)
    for pp in range(128):
        d = pp % 64
        pair = d // 2
        r = d % 2
        rope[0, pp] = cos[:, pair]
        rope[1, pp] = sin[:, pair] if r == 1 else -sin[:, pair]
    return cb.astype(NPBF), dfts, rope


def build(n_layers=DEPTH, do_prompt=True, do_sample=True, rope_dt=F32, mode=None):
    nc = bass.Bass("TRN2", target_bir_lowering=False)

    def din(name, shape, dt=F32):
        return nc.dram_tensor(name, shape, dt, kind="ExternalInput").ap()

    def dout(name, shape, dt=F32):
        return nc.dram_tensor(name, shape, dt, kind="ExternalOutput").ap()

    xpT = din("xpT", [1024, 1024])
    xsT = din("xsT", [1024, 2048])
    kcT = din("kcT", [DEPTH, 256, 256])
    vcd = din("vc", [DEPTH, 256, 256])
    W1 = din("W1", [DEPTH, 1024, 4096])
    wada = din("wada", [DEPTH, 1024, 6144])
    wfo = din("wfo", [DEPTH, 512, 1024])
    wao = din("wao", [DEPTH, 1024, 1024])
    wout = din("wout", [DEPTH, 1024, 1024])
    wff1 = din("wff1", [DEPTH, 1024, 4096])
    wff2 = din("wff2", [DEPTH, 4096, 1024])
    sfd = din("sf", [128, SF_N])
    cbd = din("cb", [128, CB_N], BF16)
    dftd = din("dfts", [2, 8, 2, 128, 2048], BF16)
    roped = din("rope", [2, 128, 2048])
    ypT = dout("ypT", [1024, 1024])
    ysT = dout("ysT", [1024, 2048])
    nkd = dout("nk", [4, DEPTH, 256, 256])
    nvd = dout("nv", [4, DEPTH, 256, 256])

    with contextlib.ExitStack() as es:
        def sb(name, shape, dt):
            return es.enter_context(nc.sbuf_tensor(name, shape, dt))

        def pst(name):
            return es.enter_context(nc.psum_tensor(name, [128, 512], F32))

        X = sb("X", [128, 8, 2048], F32)
        HT = sb("HT", [128, 8, 1024], BF16)
        B2 = sb("B2", [128, 16384], BF16)
        B4 = sb("B4", [128, 16384], BF16)
        Fb = sb("Fb", [128, 4, 1024], BF16)
        WS = [sb(f"WS{i}", [128, 2048], BF16) for i in range(NSLOT)]
        ROPE = sb("ROPE", [128, 2, 1024], rope_dt)
        PT = [sb(f"PT{i}", [128, 512], BF16) for i in range(3)]
        RSTD = sb("RSTD", [128, 512], F32)
        TMP32 = [sb(f"TMP{i}", [128, 512], F32) for i in range(3)]
        T16 = [sb(f"T16{i}", [128, 512], BF16) for i in range(4)]
        DC = [sb(f"DC{i}", [128, 256], BF16) for i in range(4)]
        DS = [sb(f"DS{i}", [128, 256], BF16) for i in range(2)]
        CB = sb("CB", [128, CB_N], BF16)
        SF = sb("SFs", [128, SF_N], F32)
        KC = sb("KC", [128, 2, 256], BF16)
        VC = sb("VC", [128, 2, 256], BF16)
        MODT = sb("MODT", [128, DEPTH, 48, 2], F32)
        GS = sb("GS", [128, DEPTH, 2, 8, 2], F32)
        ESb = sb("ES", [128, 32], F32)
        SCB = sb("SCB", [128, 8, 2], BF16)
        PSB = [pst(f"PS{i}") for i in range(8)]

        sems = {e: es.enter_context(nc.semaphore("s_" + e)) for e in Sched.ENG}
        dsems = [es.enter_context(nc.semaphore(f"d{i}")) for i in range(24)]
        block = es.enter_context(nc.Block())
        S = Sched(nc, sems, dsems)

        Qv = B2[:, 0:8192].rearrange("p (c t) -> p c t", c=8)
        ATTv = B2[:, 8192:16384].rearrange("p (c t) -> p c t", c=8)
        H2T = B2[:, :].rearrange("p (c t) -> p c t", c=8)
        Uv = B4[:, 0:8192].rearrange("p (s n) -> p s n", s=16)
        Vv = B4[:, 8192:12288].rearrange("p (s n) -> p s n", s=16)
        KTv = B4[:, 12288:16384].rearrange("p (c t) -> p c t", c=2)
        ATv = [B4[:, 0:8192].rearrange("p (c t) -> p c t", c=4),
               B4[:, 8192:16384].rearrange("p (c t) -> p c t", c=4)]

        def kX(c, tb): return ('X', c, tb)
        def kHT(c, lb): return ('HT', c, lb)
        def kQ(c, lb): return ('B2', c * 2 + lb)
        def kATT(c, lb): return ('B2', 16 + c * 2 + lb)
        def kH2(c, tb): return ('B2', c * 4 + tb)
        def kU(s): return ('B4', s)
        def kV(s): return ('B4', 16 + s // 2)
        def kKT(c, tb): return ('B4', 24 + c * 4 + tb)
        def kAT(g, kc, tb): return ('B4', g * 16 + kc * 4 + tb)
        def kF(g, lb): return ('F', g, lb)

        ONESM = CB[:, CB_ONESM:CB_ONESM + 128]
        ONES1 = CB[:, CB_ONES1:CB_ONES1 + 128]
        PSW = CB[:, CB_PSW:CB_PSW + 128]
        C128 = CB[:, CB_C128:CB_C128 + 128]
        NS128 = CB[:, CB_NS128:CB_NS128 + 128]
        MPREV = CB[:, CB_MPREV:CB_MPREV + 512]
        MNEXT = CB[:, CB_MNEXT:CB_MNEXT + 512]
        CPv = CB[:, CB_CP:CB_CP + 512].rearrange("p (s n) -> p s n", s=2)
        SPv = CB[:, CB_SP:CB_SP + 512].rearrange("p (s n) -> p s n", s=2)
        BADA = SF[:, SF_BADA:SF_BADA + 192].rearrange("p (l j) -> p l j", l=4)
        G12 = [SF[:, SF_G1:SF_G1 + 32].rearrange("p (l c) -> p l c", l=4),
               SF[:, SF_G2:SF_G2 + 32].rearrange("p (l c) -> p l c", l=4)]
        FG = SF[:, SF_FG:SF_FG + 8]
        CV = SF[:, SF_CV:SF_CV + 16].rearrange("p (c v) -> p c v", c=8)
        SINK = SF[:, SF_SINK:SF_SINK + 32]

        ctr = {'bank': 0, 'pt': 0, 'tmp': 0, 't16': 0, 'w': 0}
        NGEN = 4
        pending = []

        def defer(fn):
            pending.append(fn)

        def flush():
            while pending:
                pending.pop(0)()

        ctr['ngen'] = 7

        def bank():
            i = ctr['bank'] % ctr['ngen']
            ctr['bank'] = (i + 1) % ctr['ngen']
            return PSB[i], ('PS', i)

        def ptbuf():
            i = ctr['pt']
            ctr['pt'] = (i + 1) % len(PT)
            return PT[i], ('PT', i)

        def tmp32():
            i = ctr['tmp']
            ctr['tmp'] = (i + 1) % len(TMP32)
            return TMP32[i], ('TMP', i)

        def t16():
            i = ctr['t16']
            ctr['t16'] = (i + 1) % len(T16)
            return T16[i], ('T16', i)

        XS = [X[:, c, 1024:2048].bitcast(BF16) for c in range(8)]
        ctr['nslot'] = NSLOT

        def wload(src, kc, ncols):
            s = ctr['w'] % ctr['nslot']
            ctr['w'] = (s + 1) % ctr['nslot']
            buf = WS[s] if s < NSLOT else XS[s - NSLOT]
            view = buf[:, 0:kc * ncols].rearrange("p (k n) -> p k n", k=kc)
            S.dma('pool', f'w{s}', view, src, writes=[('W', s)])
            return view, ('W', s)

        def wload_multi(parts):
            s = ctr['w'] % ctr['nslot']
            ctr['w'] = (s + 1) % ctr['nslot']
            buf = WS[s] if s < NSLOT else XS[s - NSLOT]
            views = []
            off = 0
            for src, kc, ncols in parts:
                view = buf[:, off:off + kc * ncols].rearrange("p (k n) -> p k n", k=kc)
                S.dma('pool', f'w{s}', view, src, writes=[('W', s)])
                views.append(view)
                off += kc * ncols
            return views, ('W', s)

        def wsrc(w, l, r0, nrows, c0, ncols):
            return w[l, r0:r0 + nrows, c0:c0 + ncols].rearrange("(k p) n -> p k n", p=128)

        def chain(outap, okey, pairs, reads):
            n = len(pairs)
            fns = []
            for i, (lh, rh) in enumerate(pairs):
                fns.append(lambda e, lh=lh, rh=rh, i=i: e.matmul(outap, lh, rh, start=(i == 0), stop=(i == n - 1)))
            S.op('pe', fns, reads=reads, writes=[okey])


        def ACT(out, in_, func, reads, writes, scale=None, bias=None):
            kw = {}
            if scale is not None:
                kw['scale'] = scale
            if bias is not None:
                kw['bias'] = bias
            S.op('act', lambda e: e.activation(out=out, in_=in_, func=func, **kw), reads=reads, writes=writes)

        def TT(out, in0, in1, op, reads, writes):
            S.op('dve', lambda e: e.tensor_tensor(out=out, in0=in0, in1=in1, op=op), reads=reads, writes=writes)

        def STT(out, in0, scalar, in1, op0, op1, reads, writes):
            S.op('dve', lambda e: e.scalar_tensor_tensor(out=out, in0=in0, scalar=scalar, in1=in1, op0=op0, op1=op1), reads=reads, writes=writes)

        def TSADD(out, in0, scalar1, reads, writes):
            S.op('dve', lambda e: e.tensor_scalar(out=out, in0=in0, scalar1=scalar1, scalar2=None, op0=ALU.add), reads=reads, writes=writes)

        def COPY(out, in_, reads, writes):
            S.op('dve', lambda e: e.tensor_copy(out=out, in_=in_), reads=reads, writes=writes)

        def RECIP(out, in_, reads, writes):
            S.op('dve', lambda e: e.reciprocal(out=out, in_=in_), reads=reads, writes=writes)

        def MM(out, lh, rh, start, stop, reads, writes):
            S.op('pe', lambda e: e.matmul(out, lh, rh, start=start, stop=stop), reads=reads, writes=writes)

        S.dma('sp', 'c0', CB[:], cbd[:, :], writes=['CB'])
        S.dma('sp', 'c1', SF[:], sfd[:, :], writes=['SF'])
        S.op('act', lambda e: e.activation(out=SCB[:], in_=CV, func=AF.Silu), reads=['SF'], writes=['SCB'])
        S.op('act', lambda e: e.activation(out=ESb[:], in_=SINK, func=AF.Exp), reads=['SF'], writes=['ES'])

        def modulation_steps(l):
            bk = PSB[7]
            bkeys = [('DE', 1, 0), ('DE', 1, 1), ('PS', 7)]
            psv = bk[:, 0:96].rearrange("p (j v) -> p j v", v=2)
            steps = []

            held = {}

            def mk(js):
                def step():
                    if js > 0:
                        wv, wk = held.pop(js - 1)
                        for jj in range(2):
                            j = (js - 1) * 2 + jj
                            chain(bk[:, j * 2:j * 2 + 2], bkeys[0],
                                  [(wv[:, kc, jj * 128:(jj + 1) * 128], SCB[:, kc, :]) for kc in range(8)],
                                  [wk, 'SCB'] + (bkeys[1:] if j == 0 else []))
                    if js < 24:
                        held[js] = wload(wsrc(wada, l, 0, 1024, js * 256, 256), 8, 256)
                return step
            for js in range(25):
                steps.append(mk(js))

            def fin():
                for v in range(2):
                    TT(MODT[:, l, :, v], psv[:, :, v], BADA[:, l, :], ALU.add, bkeys + ['SF'], [('MOD', l)])
                for which in range(2):
                    off = 8 if which == 0 else 32
                    for v in range(2):
                        STT(GS[:, l, which, :, v], MODT[:, l, off:off + 8, v], 1.0, G12[which][:, l, :], ALU.add, ALU.mult,
                            [('MOD', l), 'SF'], [('GS', l)])
                S.last_w[bkeys[1]] = S.last_w[bkeys[0]]
                S.last_w[bkeys[2]] = S.last_w[bkeys[0]]
            steps.append(fin)
            return steps

        def mSH(l, which, c, v): return MODT[:, l, (0 if which == 0 else 24) + c, v:v + 1]
        def mGA(l, which, c, v): return MODT[:, l, (16 if which == 0 else 40) + c, v:v + 1]
        def mGS(l, which, c, v): return GS[:, l, which, c, v:v + 1]

        def rstd_block(t0, lb):
            tb = t0 // 512 + lb
            tok = slice(t0 + lb * 512, t0 + lb * 512 + 512)
            bk, bkk = bank()
            for c2 in range(4):
                sqs = []
                for c in (2 * c2, 2 * c2 + 1):
                    sq, sqk = t16()
                    if c % 2 == 0:
                        ACT(sq[:], X[:, c, tok], AF.Square, [kX(c, tb)], [sqk])
                    else:
                        TT(sq[:], X[:, c, tok], X[:, c, tok], ALU.mult, [kX(c, tb)], [sqk])
                    sqs.append((c, sq, sqk))
                for c, sq, sqk in sqs:
                    MM(bk[:], ONESM, sq[:], c == 0, c == 7, [sqk, 'CB'], [bkk])
            ACT(RSTD[:], bk[:], AF.Ln, [bkk], ['RSTD'], scale=1.0, bias=EPS)
            ACT(RSTD[:], RSTD[:], AF.Exp, ['RSTD'], ['RSTD'], scale=-0.5)
            return tb, tok

        def norm(l, which, v, t0, nblk, dst, dkey):
            for lb in range(nblk):
                tb, tok = rstd_block(t0, lb)
                for c in range(8):
                    tm, tmk = tmp32()
                    TT(tm[:], X[:, c, tok], RSTD[:], ALU.mult, [kX(c, tb), 'RSTD'], [tmk])
                    ACT(dst[:, c, lb * 512:(lb + 1) * 512], tm[:], AF.Identity, [tmk, ('MOD', l), ('GS', l)], [dkey(c, lb)],
                        scale=mGS(l, which, c, v), bias=mSH(l, which, c, v))

        def rope(bk, bkk, dst, dkeys, lb):
            zb, zbk = t16()
            ACT(zb[:], bk[:], AF.Identity, [bkk], [zbk])
            flush()

            def rest():
                b2, b2k = bank()
                MM(b2[:], PSW, zb[:], True, True, [zbk, 'CB'], [b2k])
                t1, t1k = tmp32()
                TT(t1[:], bk[:], ROPE[:, 0, lb * 512:(lb + 1) * 512], ALU.mult, [bkk, 'ROPE', zbk], [t1k])
                t2, t2k = tmp32()
                TT(t2[:], b2[:], ROPE[:, 1, lb * 512:(lb + 1) * 512], ALU.mult, [b2k, 'ROPE'], [t2k])
                TT(dst, t1[:], t2[:], ALU.add, [t1k, t2k], dkeys)
            defer(rest)

        def phase1(l, v, t0, sample):
            norm(l, 0, v, t0, 2, HT, kHT)
            if sample:
                for k in range(2):
                    S.dma('sp', 'rope', ROPE[:, k, :], roped[k, :, t0:t0 + 1024], writes=['ROPE'])
            if P1STOP == 'norm':
                return
            for uh in range(2):
                wv, wk = wload(wsrc(W1, l, 0, 1024, uh * 256, 256), 8, 256)
                for tt in range(8):
                    gt = t0 // 128 + tt
                    bk, bkk = bank()
                    chain(bk[:, 0:256], bkk, [(HT[:, kc, tt * 128:(tt + 1) * 128], wv[:, kc, :]) for kc in range(8)],
                          [wk] + [kHT(c, tt // 4) for c in range(8)])
                    ACT(Uv[:, gt, uh * 256:(uh + 1) * 256], bk[:, 0:256], AF.Identity, [bkk], [kU(gt)])
            if P1STOP == 'U':
                return
            wvv, wvk = wload(wsrc(W1, l, 0, 1024, 512, 256), 8, 256)
            wkv, wkk = wload(wsrc(W1, l, 0, 1024, 768, 256), 8, 256)
            for tt in range(8):
                gt = t0 // 128 + tt
                hk = [kHT(c, tt // 4) for c in range(8)]
                bk, bkk = bank()
                chain(bk[:, 0:256], bkk, [(HT[:, kc, tt * 128:(tt + 1) * 128], wvv[:, kc, :]) for kc in range(8)], [wvk] + hk)
                if sample:
                    COPY(Vv[:, gt, :], bk[:, 0:256], [bkk], [kV(gt)])
                else:
                    b2, b2k = bank()
                    chain(b2[:, 0:256], b2k, [(HT[:, kc, tt * 128:(tt + 1) * 128], wkv[:, kc, :]) for kc in range(8)], [wkk] + hk)
                    st, stk = tmp32()
                    ACT(st[:, 0:256], bk[:, 0:256], AF.Identity, [bkk], [stk])
                    ACT(st[:, 256:512], b2[:, 0:256], AF.Identity, [b2k], [stk])
                    COPY(Vv[:, gt, :], st[:, 0:256], [stk], [kV(gt)])
                    b = tt // 2
                    s0 = (tt % 2) * 128
                    if P1STOP not in ('Vnodma', 'Vk'):
                        S.dma('sp', 'out', nvd[b, l, s0:s0 + 128, :], st[:, 0:256], reads=[stk], writes=[('OUT', 'nv', l, tt)])
                        S.dma('sp', 'out', nkd[b, l, s0:s0 + 128, :], st[:, 256:512], reads=[stk], writes=[('OUT', 'nk', l, tt)])
            if P1STOP in ('V', 'Vnodma', 'Vonly', 'Vk'):
                return
            for c in range(2):
                for lb in range(2):
                    tb = t0 // 512 + lb
                    bk, bkk = bank()
                    chain(bk[:], bkk, [(wkv[:, kc, c * 128:(c + 1) * 128], HT[:, kc, lb * 512:(lb + 1) * 512]) for kc in range(8)],
                          [wkk] + [kHT(cc, lb) for cc in range(8)])
                    dst = KTv[:, c, t0 + lb * 512:t0 + lb * 512 + 512]
                    if sample:
                        rope(bk, bkk, dst, [kKT(c, tb)], lb)
                    else:
                        COPY(dst, bk[:], [bkk], [kKT(c, tb)])
            flush()

        def fourier(l, t0, sample, hook=None):
            for qi in range(4):
                if qi == 1 and hook is not None:
                    hook()
                if sample:
                    qu = t0 // 256 + qi
                    slots = []
                    for k in range(2):
                        for sh in range(2):
                            slots.append(wload(dftd[k, qu, sh].rearrange("p (s n) -> p s n", s=8), 8, 256))
                    nst = 16
                    rcl = [(slots[st // 8][0][:, st % 8, :], slots[st // 8][1]) for st in range(16)]
                    rsl = [(slots[2 + st // 8][0][:, st % 8, :], slots[2 + st // 8][1]) for st in range(16)]
                    ut0 = 0
                else:
                    nst = 2
                    rcl = [(CPv[:, st, :], 'CB') for st in range(2)]
                    rsl = [(SPv[:, st, :], 'CB') for st in range(2)]
                    ut0 = qi * 2
                for g in range(4):
                    bk, bkk = bank()
                    prs = [(Uv[:, ut0 + st, g * 128:(g + 1) * 128], rcl[st][0]) for st in range(nst)]
                    rk = list({rcl[st][1] for st in range(nst)}) + [kU(ut0 + st) for st in range(nst)]
                    chain(bk[:, 0:256], bkk, prs, rk)
                    COPY(DC[g][:], bk[:, 0:256], [bkk], [('DC', g)])
                    if g == 0:
                        flush()
                for g in range(4):
                    bk, bkk = bank()
                    prs = [(Uv[:, ut0 + st, g * 128:(g + 1) * 128], rsl[st][0]) for st in range(nst)]
                    rk = list({rsl[st][1] for st in range(nst)}) + [kU(ut0 + st) for st in range(nst)]
                    chain(bk[:, 0:256], bkk, prs, rk)
                    COPY(DS[g % 2][:], bk[:, 0:256], [bkk], [('DS', g % 2)])
                    flush()

                    def rest(g=g, qi=qi):
                        b2, b2k = bank()
                        chain(b2[:, 0:256], b2k, [(C128, DC[g][:]), (NS128, DS[g % 2][:])], ['CB', ('DC', g), ('DS', g % 2)])
                        ACT(Fb[:, g, qi * 256:(qi + 1) * 256], b2[:, 0:256], AF.Identity, [b2k], [('Fq', g, qi)])
                    defer(rest)
            flush()

        def kFr(g, lb): return [('Fq', g, lb * 2), ('Fq', g, lb * 2 + 1)]

        def qnorm(l, v, t0):
            for k in range(2):
                S.dma('sp', 'rope', ROPE[:, k, :], roped[k, :, t0:t0 + 1024], writes=['ROPE'])
            norm(l, 0, v, t0, 2, HT, kHT)

        def qproj(l, v, t0, sample):
            for cs in range(4):
                wv, wk = wload(wsrc(W1, l, 0, 1024, 1024 + cs * 256, 256), 8, 256)
                for cc in range(2):
                    c = cs * 2 + cc
                    for lb in range(2):
                        bk, bkk = bank()
                        chain(bk[:], bkk, [(wv[:, kc, cc * 128:(cc + 1) * 128], HT[:, kc, lb * 512:(lb + 1) * 512]) for kc in range(8)],
                              [wk] + [kHT(k2, lb) for k2 in range(8)])
                        dst = Qv[:, c, lb * 512:(lb + 1) * 512]
                        if sample:
                            rope(bk, bkk, dst, [kQ(c, lb)], lb)
                        else:
                            ACT(dst, bk[:], AF.Identity, [bkk], [kQ(c, lb)])
            flush()

        actr = {'n': 0}
        att_pend = []

        def attention(l, t0, sample):
            ctr['ngen'] = 4
            for i in range(8):
                gi = t0 // 128 + i
                if sample:
                    keys = []
                    if gi > 0:
                        keys.append((0, gi - 1, MPREV))
                    keys.append((0, gi, None))
                    if gi < 15:
                        keys.append((0, gi + 1, MNEXT))
                    keys += [(1, 0, None), (1, 1, None)]
                else:
                    sq = i // 2
                    keys = [(0, 2 * sq, None), (0, 2 * sq + 1, None)]
                nk = len(keys)
                lb = i // 4
                for gp in range(2):
                    par = actr['n'] % 2
                    actr['n'] += 1
                    OBs = [PSB[4 + 2 * par], PSB[4 + 2 * par]]
                    DENs = [PSB[5 + 2 * par], PSB[5 + 2 * par]]
                    qk = {}

                    def emit_qk(idx):
                        src, kt, msk = keys[idx]
                        for gg in range(2):
                            pb = gg * 64
                            bk, bkk = bank()
                            if src == 0:
                                lh = KTv[pb:pb + 64, gp, kt * 128:(kt + 1) * 128]
                                lk = kKT(gp, kt // 4)
                            else:
                                lh = KC[pb:pb + 64, gp, kt * 128:(kt + 1) * 128]
                                lk = 'KC'
                            rh = Qv[pb:pb + 64, gp * 4:(gp + 1) * 4, i * 128:(i + 1) * 128]
                            MM(bk[:], lh, rh, True, True, [lk] + [kQ(gp * 4 + j, lb) for j in range(4)], [bkk])
                            qk[(idx, gg)] = (bk, bkk)

                    emit_qk(0)
                    if nk > 1:
                        emit_qk(1)
                    for idx in range(nk):
                        src, kt, msk = keys[idx]
                        pts = []
                        for gg in range(2):
                            bk, bkk = qk[(idx, gg)]
                            pt, ptk = ptbuf()
                            ACT(pt[:], bk[:], AF.Exp, [bkk], [ptk], scale=0.125)
                            if msk is not None:
                                TT(pt[:], pt[:], msk, ALU.mult, [ptk, 'CB'], [ptk])
                            pts.append((pt, ptk))
                        if idx == min(1, nk - 1):
                            while att_pend:
                                att_pend.pop(0)()
                        for gg in range(2):
                            g = gp * 2 + gg
                            pb = gg * 64
                            pt, ptk = pts[gg]
                            if src == 0:
                                vv = Vv[:, kt, g * 64:(g + 1) * 64]
                                vk = kV(kt)
                            else:
                                vv = VC[:, kt, g * 64:(g + 1) * 64]
                                vk = 'VC'
                            MM(OBs[gg][pb:pb + 64, :], vv, pt[:], idx == 0, idx == nk - 1, [vk, ptk], [('OB', par, gg), ('PS', 4 + 2 * par)])
                        for gg in range(2):
                            pb = gg * 64
                            pt, ptk = pts[gg]
                            MM(DENs[gg][pb:pb + 64, :], ONES1[:, 0:64], pt[:], idx == 0, idx == nk - 1, [ptk, 'CB'], [('DE', par, gg), ('PS', 5 + 2 * par)])
                        if idx + 2 < nk:
                            emit_qk(idx + 2)
                    def norm_att(OB=OBs[0], DEN=DENs[0], par=par, gp=gp, i=i, lb=lb):
                        ds_, dsk = tmp32()
                        for j in range(4):
                            TSADD(ds_[:, j * 128:(j + 1) * 128], DEN[:, j * 128:(j + 1) * 128],
                                  ESb[:, l * 8 + gp * 4 + j:l * 8 + gp * 4 + j + 1], [('DE', par, 0), ('DE', par, 1), ('PS', 5 + 2 * par), 'ES'], [dsk])
                        ACT(ds_[:], ds_[:], AF.Ln, [dsk], [dsk])
                        ACT(ds_[:], ds_[:], AF.Exp, [dsk], [dsk], scale=-1.0)
                        TT(ATTv[:, gp * 4:(gp + 1) * 4, i * 128:(i + 1) * 128],
                           OB[:, :].rearrange("p (j q) -> p j q", j=4),
                           ds_[:, :].rearrange("p (j q) -> p j q", j=4), ALU.mult,
                           [('OB', par, 0), ('OB', par, 1), ('PS', 4 + 2 * par), dsk],
                           [('ATTp', gp * 4 + j, lb, i) for j in range(4)] + [kATT(gp * 4 + j, lb) for j in range(4)])
                    att_pend.append(norm_att)
            while att_pend:
                att_pend.pop(0)()
            ctr['ngen'] = 7

        def kATTr(c, lb):
            return [('ATTp', c, lb, i) for i in range(lb * 4, lb * 4 + 4)] + [kATT(c, lb)]

        def merge(l, v, t0):
            for c in range(8):
                (fo, ao), fak = wload_multi([(wsrc(wfo, l, 0, 512, c * 128, 128), 4, 128),
                                             (wsrc(wao, l, 0, 1024, c * 128, 128), 8, 128)])
                (gf, ga), ggk = wload_multi([(wsrc(W1, l, 0, 1024, 2048 + c * 128, 128), 8, 128),
                                             (wsrc(W1, l, 0, 1024, 3072 + c * 128, 128), 8, 128)])
                for lb in range(2):
                    ts = slice(lb * 512, (lb + 1) * 512)
                    bgf, bgfk = bank()
                    chain(bgf[:], bgfk, [(gf[:, kc, :], HT[:, kc, ts]) for kc in range(8)], [ggk] + [kHT(k2, lb) for k2 in range(8)])
                    sf, sfk = tmp32()
                    ACT(sf[:], bgf[:], AF.Sigmoid, [bgfk], [sfk])
                    bfo, bfok = bank()
                    chain(bfo[:], bfok, [(fo[:, kc, :], Fb[:, kc, ts]) for kc in range(4)],
                          [fak] + [k for g in range(4) for k in kFr(g, lb)])
                    TT(sf[:], bfo[:], sf[:], ALU.mult, [bfok, sfk], [sfk])
                    bga, bgak = bank()
                    chain(bga[:], bgak, [(ga[:, kc, :], HT[:, kc, ts]) for kc in range(8)], [ggk] + [kHT(k2, lb) for k2 in range(8)])
                    sa, sak = tmp32()
                    ACT(sa[:], bga[:], AF.Sigmoid, [bgak], [sak])
                    bao, baok = bank()
                    chain(bao[:], baok, [(ao[:, kc, :], ATTv[:, kc, ts]) for kc in range(8)],
                          [fak] + [k for k2 in range(8) for k in kATTr(k2, lb)])
                    TT(sa[:], bao[:], sa[:], ALU.mult, [baok, sak], [sak])
                    TT(Qv[:, c, ts], sf[:], sa[:], ALU.add, [sfk, sak], [kQ(c, lb)])

        def wo(l, v, t0):
            for cp in range(4):
                wv, wk = wload(wsrc(wout, l, 0, 1024, cp * 256, 256), 8, 256)
                for cc in range(2):
                    c = cp * 2 + cc
                    for lb in range(2):
                        tb = t0 // 512 + lb
                        tok = slice(t0 + lb * 512, t0 + lb * 512 + 512)
                        bk, bkk = bank()
                        chain(bk[:], bkk, [(wv[:, kc, cc * 128:(cc + 1) * 128], Qv[:, kc, lb * 512:(lb + 1) * 512]) for kc in range(8)],
                              [wk] + [kQ(k2, lb) for k2 in range(8)])
                        STT(X[:, c, tok], bk[:], mGA(l, 0, c, v), X[:, c, tok], ALU.mult, ALU.add,
                            [bkk, kX(c, tb), ('MOD', l)], [kX(c, tb)])

        def mlp(l, v, nblk, hook=None):
            norm(l, 1, v, 0, nblk, H2T, kH2)
            for hg in range(8):
                ab = hg % 2
                for sh in range(2):
                    if hook is not None:
                        hook()
                    wv, wk = wload(wsrc(wff1, l, 0, 1024, hg * 512 + sh * 256, 256), 8, 256)
                    for cc in range(2):
                        kcl = sh * 2 + cc
                        for lb in range(nblk):
                            ts = slice(lb * 512, (lb + 1) * 512)
                            bk, bkk = bank()
                            chain(bk[:], bkk, [(wv[:, kc, cc * 128:(cc + 1) * 128], H2T[:, kc, ts]) for kc in range(8)],
                                  [wk] + [kH2(k2, lb) for k2 in range(8)])
                            tm, tmk = tmp32()
                            ACT(tm[:], bk[:], AF.Relu, [bkk], [tmk])
                            TT(ATv[ab][:, kcl, ts], tm[:], tm[:], ALU.mult, [tmk], [kAT(ab, kcl, lb)])
                for oh in range(2):
                    if hook is not None:
                        hook()
                    wv, wk = wload(wsrc(wff2, l, hg * 512, 512, oh * 512, 512), 4, 512)
                    for cc in range(4):
                        c = oh * 4 + cc
                        for lb in range(nblk):
                            ts = slice(lb * 512, (lb + 1) * 512)
                            bk, bkk = bank()
                            chain(bk[:], bkk, [(wv[:, kc, cc * 128:(cc + 1) * 128], ATv[ab][:, kc, ts]) for kc in range(4)],
                                  [wk] + [kAT(ab, k2, lb) for k2 in range(4)])
                            STT(X[:, c, ts], bk[:], mGA(l, 1, c, v), X[:, c, ts], ALU.mult, ALU.add,
                                [bkk, kX(c, lb), ('MOD', l)], [kX(c, lb)])

        def final(nblk, outd):
            for lb in range(nblk):
                tb, tok = rstd_block(0, lb)
                for c in range(8):
                    tm, tmk = tmp32()
                    TT(tm[:], X[:, c, tok], RSTD[:], ALU.mult, [kX(c, tb), 'RSTD'], [tmk])
                    ACT(tm[:], tm[:], AF.Identity, [tmk, 'SF'], [tmk], scale=FG[:, c:c + 1])
                    S.dma('sp', 'out', outd[c * 128:(c + 1) * 128, tok], tm[:], reads=[tmk], writes=[('OUT', 'y', c, lb)])

        def load_x(xd, T):
            for c in range(8):
                extra = [('W', NSLOT + c)] if T > 1024 else []
                S.dma('sp', 'x', X[:, c, 0:T], xd[c * 128:(c + 1) * 128, :], writes=[kX(c, tb) for tb in range(T // 512)] + extra)

        if do_prompt:
            load_x(xpT, 1024)
            ctr['nslot'] = NSLOT + 8
        for l in range(n_layers):
            if l == 0 or not do_prompt:
                for st in modulation_steps(l):
                    st()
            nxt = modulation_steps(l + 1) if (do_prompt and l + 1 < n_layers) else []

            def mhook(nxt=nxt):
                if len(nxt) > 1:
                    nxt.pop(0)()
            if do_prompt:
                if _on('p1'): phase1(l, 0, 0, False)
                if _on('fourier'): fourier(l, 0, False)
                if _on('qproj'): qproj(l, 0, 0, False)
                if _on('attn'): attention(l, 0, False)
                if _on('merge'): merge(l, 0, 0)
                if _on('wo'): wo(l, 0, 0)
                if _on('mlp'): mlp(l, 0, 2, hook=mhook)
            while nxt:
                nxt.pop(0)()
        if mode == 'io':
          for rep in range(10):
            for c in range(8):
                S.dma('sp', 'out', ypT[c * 128:(c + 1) * 128, :], X[:, c, 0:1024], reads=[kX(c, 0), kX(c, 1)], writes=[('OUT', c, rep)])
        elif mode == 'sq':
            for c in range(8):
                tm, tmk = tmp32()
                ACT(tm[:], X[:, c, 0:512], AF.Square, [kX(c, 0)], [tmk])
                S.dma('sp', 'out', ypT[c * 128:(c + 1) * 128, 0:512], tm[:], reads=[tmk], writes=[('OUT', c)])
        elif do_prompt:
            final(2, ypT)
        ctr['nslot'] = NSLOT
        ctr['w'] = 0
        if do_sample:
            load_x(xsT, 2048)
            for l in range(n_layers):
                S.dma('pool', 'kc', KC[:], kcT[l].rearrange("(c p) t -> p c t", p=128), writes=['KC'])
                S.dma('pool', 'vc', VC[:], vcd[l].rearrange("(s p) n -> p s n", p=128), writes=['VC'])
                for hf in range(2):
                    if _on('p1'): phase1(l, 1, hf * 1024, True)
                for hf in range(2):
                    t0 = hf * 1024
                    if _on('fourier'): fourier(l, t0, True, hook=(lambda l=l, t0=t0: qnorm(l, 1, t0)))
                    if _on('qproj'): qproj(l, 1, t0, True)
                    if _on('attn'): attention(l, t0, True)
                    if _on('merge'): merge(l, 1, t0)
                    if _on('wo'): wo(l, 1, t0)
                if _on('mlp'): mlp(l, 1, 4)
            final(4, ysT)
        S.wait_all('sp', 'out')

        @block.sync
        def _(e):
            S.replay('sp', e)

        @block.scalar
        def _(e):
            S.replay('act', e)

        @block.vector
        def _(e):
            S.replay('dve', e)

        @block.gpsimd
        def _(e):
            S.replay('pool', e)

        @block.tensor
        def _(e):
            S.replay('pe', e)
    return nc, S


_CONST = None


def _prep(inputs):
    global _CONST
    if _CONST is None:
        _CONST = _const_tables()
    cb, dfts, rope = _CONST
    f = lambda a: np.ascontiguousarray(np.asarray(a, dtype=np.float32))
    w_in = f(inputs['w_in'])
    hp = _head_perm()
    qcols = np.concatenate([np.arange(512 + h * 64, 512 + (h + 1) * 64) for h in hp])
    W1 = np.ascontiguousarray(np.concatenate(
        [w_in[:, :, 0:512], w_in[:, :, 1792:2048], w_in[:, :, 1536:1792], w_in[:, :, qcols], w_in[:, :, 2048:4096]], axis=2))
    wao = np.ascontiguousarray(f(inputs['w_ao'])[:, qcols - 512, :])
    shared = dict(W1=W1, wada=f(inputs['w_ada']), wfo=f(inputs['w_fo']), wao=wao, wout=f(inputs['w_out']),
                  wff1=f(inputs['w_ff1']), wff2=f(inputs['w_ff2']), cb=cb, dfts=dfts, rope=rope)
    x_prompt = f(inputs['x_prompt'])
    x_sample = f(inputs['x_sample'])
    c = f(inputs['c'])
    c_ctx = f(inputs['c_ctx'])
    ck = f(inputs['cache_k'])
    cv = f(inputs['cache_v'])
    b_ada = f(inputs['b_ada'])
    sfbase = np.zeros((128, SF_N), np.float32)
    sfbase[:, SF_BADA:SF_BADA + 192] = b_ada.reshape(4, 48, 128).transpose(2, 0, 1).reshape(128, 192)
    sfbase[:, SF_G1:SF_G1 + 32] = f(inputs['norm1_g']).reshape(4, 8, 128).transpose(2, 0, 1).reshape(128, 32)
    sfbase[:, SF_G2:SF_G2 + 32] = f(inputs['norm2_g']).reshape(4, 8, 128).transpose(2, 0, 1).reshape(128, 32)
    sfbase[:, SF_FG:SF_FG + 8] = f(inputs['final_g']).reshape(8, 128).T
    snk = f(inputs['sink']).reshape(4, 2, 2, 4)
    sfbase[0:64, SF_SINK:SF_SINK + 32] = np.broadcast_to(snk[:, :, 0, :].reshape(1, 32), (64, 32))
    sfbase[64:128, SF_SINK:SF_SINK + 32] = np.broadcast_to(snk[:, :, 1, :].reshape(1, 32), (64, 32))
    in_maps = []
    for i in range(8):
        m = dict(shared)
        m['xpT'] = np.ascontiguousarray(x_prompt[4 * i:4 * i + 4].reshape(1024, 1024).T)
        m['xsT'] = np.ascontiguousarray(x_sample[i].T)
        m['kcT'] = np.ascontiguousarray(ck[i].reshape(4, 256, 256).transpose(0, 2, 1))
        m['vc'] = np.ascontiguousarray(cv[i].reshape(4, 256, 256))
        sfa = sfbase.copy()
        cvv = np.stack([c_ctx, c[i]], axis=0)
        sfa[:, SF_CV:SF_CV + 16] = cvv.reshape(2, 8, 128).transpose(2, 1, 0).reshape(128, 16)
        m['sf'] = sfa
        in_maps.append(m)
    return in_maps


_NC = None


def kernel(**inputs):
    global _NC
    in_maps = _prep(inputs)
    if _NC is None:
        _NC = build()[0]
    res = run_bass_kernel_spmd(_NC, in_maps, core_ids=list(range(8)))
    y_prompt = np.zeros((32, 256, 1024), np.float32)
    y_sample = np.zeros((8, 2048, 1024), np.float32)
    nk = np.zeros((32, 4, 256, 4, 64), np.float32)
    nv = np.zeros((32, 4, 256, 4, 64), np.float32)
    for i in range(8):
        r = res.results[i]
        y_prompt[4 * i:4 * i + 4] = r['ypT'].T.reshape(4, 256, 1024)
        y_sample[i] = r['ysT'].T
        nk[4 * i:4 * i + 4] = r['nk'].reshape(4, 4, 256, 4, 64)
        nv[4 * i:4 * i + 4] = r['nv'].reshape(4, 4, 256, 4, 64)
    return (y_prompt, y_sample, nk, nv)
```

```python
import contextlib
import numpy as np
import ml_dtypes
import concourse.bass as bass
import concourse.mybir as mybir
from concourse.bass_utils import run_bass_kernel_spmd

F32 = mybir.dt.float32
BF16 = mybir.dt.bfloat16
AF = mybir.ActivationFunctionType
ALU = mybir.AluOpType
NPBF = ml_dtypes.bfloat16

DEPTH = 4
EPS = 1e-6
NSLOT = 5
STOP = None
P1STOP = None
ORDER = ['mod', 'p1', 'fourier', 'qproj', 'attn', 'merge', 'wo', 'mlp']


def _on(name):
    return STOP is None or ORDER.index(name) <= ORDER.index(STOP)


class Sched:
    ENG = ('pe', 'act', 'dve', 'pool', 'sp')

    def __init__(self, nc, sems, dma_sems):
        self.nc = nc
        self.q = {e: [] for e in self.ENG}
        self.sem = dict(sems)
        self.free_dma_sems = list(dma_sems)
        self.cnt = {}
        self.seen = {e: {} for e in self.ENG}
        self.last_w = {}
        self.readers = {}

    def _deps(self, engine, reads, writes):
        deps = {}

        def add(cv):
            if cv is None:
                return
            c, v = cv
            if c == engine and engine == 'pe':
                return
            if deps.get(c, 0) < v:
                deps[c] = v
        for r in reads:
            add(self.last_w.get(r))
        for w in writes:
            add(self.last_w.get(w))
            for c, v in self.readers.get(w, {}).items():
                add((c, v))
        return deps

    def _emit_waits(self, engine, deps):
        for c, v in deps.items():
            if c.startswith('dma:'):
                v = self.cnt[c]
            if self.seen[engine].get(c, 0) >= v:
                continue
            self.seen[engine][c] = v
            self.q[engine].append(('wait', self.sem[c], v))

    def _mark(self, counter, val, reads, writes):
        for r in reads:
            d = self.readers.setdefault(r, {})
            if d.get(counter, 0) < val:
                d[counter] = val
        for w in writes:
            self.last_w[w] = (counter, val)
            self.readers[w] = {}

    def op(self, engine, fns, reads=(), writes=()):
        if callable(fns):
            fns = [fns]
        deps = self._deps(engine, reads, writes)
        self._emit_waits(engine, deps)
        self.cnt[engine] = self.cnt.get(engine, 0) + 1
        val = self.cnt[engine]
        for f in fns[:-1]:
            self.q[engine].append(('ins', f, None))
        self.q[engine].append(('ins', fns[-1], (self.sem[engine], 1)))
        self._mark(engine, val, reads, writes)

    def dma(self, queue, semname, out, in_, reads=(), writes=()):
        c = 'dma:' + semname
        if c not in self.sem:
            self.sem[c] = self.free_dma_sems.pop()
        deps = self._deps(queue, reads, writes)
        self._emit_waits(queue, deps)
        self.cnt[c] = self.cnt.get(c, 0) + 16
        val = self.cnt[c]
        self.q[queue].append(('ins', (lambda e, out=out, in_=in_: e.dma_start(out=out, in_=in_)),
                              (self.sem[c], 16)))
        self._mark(c, val, reads, writes)

    def wait_all(self, engine, semname):
        c = 'dma:' + semname
        if c in self.cnt:
            self.q[engine].append(('wait', self.sem[c], self.cnt[c]))

    def replay(self, engine, e):
        for item in self.q[engine]:
            if item[0] == 'wait':
                e.wait_ge(item[1], item[2])
            else:
                ins = item[1](e)
                if item[2] is not None:
                    ins.then_inc(item[2][0], item[2][1])


CB_ONESM, CB_ONES1, CB_PSW, CB_C128, CB_NS128 = 0, 128, 256, 384, 512
CB_MPREV, CB_MNEXT, CB_CP, CB_SP, CB_N = 640, 1152, 1664, 2176, 2688
SF_BADA, SF_G1, SF_G2, SF_FG, SF_CV, SF_SINK, SF_N = 0, 192, 224, 256, 264, 280, 312


def _head_perm():
    hp = []
    for half in range(2):
        for j in range(4):
            hp += [half * 8 + j, half * 8 + 4 + j]
    return hp


def _const_tables():
    cb = np.zeros((128, CB_N), np.float32)
    cb[:, CB_ONESM:CB_ONESM + 128] = 1.0 / 1024.0
    cb[:, CB_ONES1:CB_ONES1 + 128] = 1.0
    p = np.arange(128)
    psw = np.zeros((128, 128), np.float32)
    psw[p, p ^ 1] = 1.0
    cb[:, CB_PSW:CB_PSW + 128] = psw
    ang = 2.0 * np.pi * np.outer(p, p) / 128.0
    cb[:, CB_C128:CB_C128 + 128] = np.cos(ang) / np.sqrt(128.0)
    cb[:, CB_NS128:CB_NS128 + 128] = -np.sin(ang) / np.sqrt(128.0)
    a = p[:, None]
    b = p[None, :]
    cb[:, CB_MPREV:CB_MPREV + 512] = np.tile((a >= b).astype(np.float32), (1, 4))
    cb[:, CB_MNEXT:CB_MNEXT + 512] = np.tile((a <= b).astype(np.float32), (1, 4))
    s = np.arange(256)
    ang = 2.0 * np.pi * np.outer(s, s) / 256.0
    c256 = (np.cos(ang) / 16.0).reshape(2, 128, 256).transpose(1, 0, 2).reshape(128, 512)
    s256 = (np.sin(ang) / 16.0).reshape(2, 128, 256).transpose(1, 0, 2).reshape(128, 512)
    cb[:, CB_CP:CB_CP + 512] = c256
    cb[:, CB_SP:CB_SP + 512] = s256
    s = np.arange(2048, dtype=np.float64)
    idx = (np.outer(s, s) % 2048)
    tab_c = np.cos(2.0 * np.pi * np.arange(2048) / 2048.0) / np.sqrt(2048.0)
    tab_s = np.sin(2.0 * np.pi * np.arange(2048) / 2048.0) / np.sqrt(2048.0)
    idx = idx.astype(np.int64)
    out = np.zeros((2, 8, 2, 128, 8, 256), np.float32)
    for k, tab in enumerate((tab_c, tab_s)):
        m = tab[idx].astype(np.float32)
        m = m.reshape(2, 8, 128, 8, 256)
        out[k] = m.transpose(3, 0, 2, 1, 4)
    dfts = out.reshape(2, 8, 2, 128, 2048).astype(NPBF)
    t = np.arange(2048)
    row = (t // 64).astype(np.float32)
    col = (t % 64).astype(np.float32)
    inv = (10000.0 ** (-np.arange(16, dtype=np.float32) / 16.0)).astype(np.float32)
    angs = np.concatenate([row[:, None] * inv[None, :], col[:, None] * inv[None, :]], axis=-1)
    cos = np.cos(angs).astype(np.float32)
    sin = np.sin(angs).astype(np.float32)
    rope = np.zeros((2, 128, 2048), np.float32)
    for pp in range(128):
        d = pp % 64
        pair = d // 2
        r = d % 2
        rope[0, pp] = cos[:, pair]
        rope[1, pp] = sin[:, pair] if r == 1 else -sin[:, pair]
    return cb.astype(NPBF), dfts, rope


def build(n_layers=DEPTH, do_prompt=True, do_sample=True, rope_dt=F32, mode=None):
    nc = bass.Bass("TRN2", target_bir_lowering=False)

    def din(name, shape, dt=F32):
        return nc.dram_tensor(name, shape, dt, kind="ExternalInput").ap()

    def dout(name, shape, dt=F32):
        return nc.dram_tensor(name, shape, dt, kind="ExternalOutput").ap()

    xpT = din("xpT", [1024, 1024])
    xsT = din("xsT", [1024, 2048])
    kcT = din("kcT", [DEPTH, 256, 256])
    vcd = din("vc", [DEPTH, 256, 256])
    W1 = din("W1", [DEPTH, 1024, 4096])
    wada = din("wada", [DEPTH, 1024, 6144])
    wfo = din("wfo", [DEPTH, 512, 1024])
    wao = din("wao", [DEPTH, 1024, 1024])
    wout = din("wout", [DEPTH, 1024, 1024])
    wff1 = din("wff1", [DEPTH, 1024, 4096])
    wff2 = din("wff2", [DEPTH, 4096, 1024])
    sfd = din("sf", [128, SF_N])
    cbd = din("cb", [128, CB_N], BF16)
    dftd = din("dfts", [2, 8, 2, 128, 2048], BF16)
    roped = din("rope", [2, 128, 2048])
    ypT = dout("ypT", [1024, 1024])
    ysT = dout("ysT", [1024, 2048])
    nkd = dout("nk", [4, DEPTH, 256, 256])
    nvd = dout("nv", [4, DEPTH, 256, 256])

    with contextlib.ExitStack() as es:
        def sb(name, shape, dt):
            return es.enter_context(nc.sbuf_tensor(name, shape, dt))

        def pst(name):
            return es.enter_context(nc.psum_tensor(name, [128, 512], F32))

        X = sb("X", [128, 8, 2048], F32)
        HT = sb("HT", [128, 8, 1024], BF16)
        B2 = sb("B2", [128, 16384], BF16)
        B4 = sb("B4", [128, 16384], BF16)
        Fb = sb("Fb", [128, 4, 1024], BF16)
        WS = [sb(f"WS{i}", [128, 2048], BF16) for i in range(NSLOT)]
        ROPE = sb("ROPE", [128, 2, 1024], rope_dt)
        PT = [sb(f"PT{i}", [128, 512], BF16) for i in range(3)]
        RSTD = sb("RSTD", [128, 512], F32)
        TMP32 = [sb(f"TMP{i}", [128, 512], F32) for i in range(3)]
        T16 = [sb(f"T16{i}", [128, 512], BF16) for i in range(3)]
        DC = [sb(f"DC{i}", [128, 256], BF16) for i in range(4)]
        DS = [sb(f"DS{i}", [128, 256], BF16) for i in range(2)]
        CB = sb("CB", [128, CB_N], BF16)
        SF = sb("SFs", [128, SF_N], F32)
        KC = sb("KC", [128, 2, 256], BF16)
        VC = sb("VC", [128, 2, 256], BF16)
        MODT = sb("MODT", [128, DEPTH, 48, 2], F32)
        GS = sb("GS", [128, DEPTH, 2, 8, 2], F32)
        ESb = sb("ES", [128, 32], F32)
        SCB = sb("SCB", [128, 8, 2], BF16)
        PSB = [pst(f"PS{i}") for i in range(8)]

        sems = {e: es.enter_context(nc.semaphore("s_" + e)) for e in Sched.ENG}
        dsems = [es.enter_context(nc.semaphore(f"d{i}")) for i in range(24)]
        block = es.enter_context(nc.Block())
        S = Sched(nc, sems, dsems)

        Qv = B2[:, 0:8192].rearrange("p (c t) -> p c t", c=8)
        ATTv = B2[:, 8192:16384].rearrange("p (c t) -> p c t", c=8)
        H2T = B2[:, :].rearrange("p (c t) -> p c t", c=8)
        Uv = B4[:, 0:8192].rearrange("p (s n) -> p s n", s=16)
        Vv = B4[:, 8192:12288].rearrange("p (s n) -> p s n", s=16)
        KTv = B4[:, 12288:16384].rearrange("p (c t) -> p c t", c=2)
        ATv = [B4[:, 0:8192].rearrange("p (c t) -> p c t", c=4),
               B4[:, 8192:16384].rearrange("p (c t) -> p c t", c=4)]

        def kX(c, tb): return ('X', c, tb)
        def kHT(c, lb): return ('HT', c, lb)
        def kQ(c, lb): return ('B2', c * 2 + lb)
        def kATT(c, lb): return ('B2', 16 + c * 2 + lb)
        def kH2(c, tb): return ('B2', c * 4 + tb)
        def kU(s): return ('B4', s)
        def kV(s): return ('B4', 16 + s // 2)
        def kKT(c, tb): return ('B4', 24 + c * 4 + tb)
        def kAT(g, kc, tb): return ('B4', g * 16 + kc * 4 + tb)
        def kF(g, lb): return ('F', g, lb)

        ONESM = CB[:, CB_ONESM:CB_ONESM + 128]
        ONES1 = CB[:, CB_ONES1:CB_ONES1 + 128]
        PSW = CB[:, CB_PSW:CB_PSW + 128]
        C128 = CB[:, CB_C128:CB_C128 + 128]
        NS128 = CB[:, CB_NS128:CB_NS128 + 128]
        MPREV = CB[:, CB_MPREV:CB_MPREV + 512]
        MNEXT = CB[:, CB_MNEXT:CB_MNEXT + 512]
        CPv = CB[:, CB_CP:CB_CP + 512].rearrange("p (s n) -> p s n", s=2)
        SPv = CB[:, CB_SP:CB_SP + 512].rearrange("p (s n) -> p s n", s=2)
        BADA = SF[:, SF_BADA:SF_BADA + 192].rearrange("p (l j) -> p l j", l=4)
        G12 = [SF[:, SF_G1:SF_G1 + 32].rearrange("p (l c) -> p l c", l=4),
               SF[:, SF_G2:SF_G2 + 32].rearrange("p (l c) -> p l c", l=4)]
        FG = SF[:, SF_FG:SF_FG + 8]
        CV = SF[:, SF_CV:SF_CV + 16].rearrange("p (c v) -> p c v", c=8)
        SINK = SF[:, SF_SINK:SF_SINK + 32]

        ctr = {'bank': 0, 'pt': 0, 'tmp': 0, 't16': 0, 'w': 0}
        NGEN = 4
        pending = []

        def defer(fn):
            pending.append(fn)

        def flush():
            while pending:
                pending.pop(0)()

        ctr['ngen'] = 7

        def bank():
            i = ctr['bank'] % ctr['ngen']
            ctr['bank'] = (i + 1) % ctr['ngen']
            return PSB[i], ('PS', i)

        def ptbuf():
            i = ctr['pt']
            ctr['pt'] = (i + 1) % len(PT)
            return PT[i], ('PT', i)

        def tmp32():
            i = ctr['tmp']
            ctr['tmp'] = (i + 1) % len(TMP32)
            return TMP32[i], ('TMP', i)

        def t16():
            i = ctr['t16']
            ctr['t16'] = (i + 1) % len(T16)
            return T16[i], ('T16', i)

        XS = [X[:, c, 1024:2048].bitcast(BF16) for c in range(8)]
        ctr['nslot'] = NSLOT

        def wload(src, kc, ncols):
            s = ctr['w'] % ctr['nslot']
            ctr['w'] = (s + 1) % ctr['nslot']
            buf = WS[s] if s < NSLOT else XS[s - NSLOT]
            view = buf[:, 0:kc * ncols].rearrange("p (k n) -> p k n", k=kc)
            S.dma('pool', f'w{s}', view, src, writes=[('W', s)])
            return view, ('W', s)

        def wload_multi(parts):
            s = ctr['w'] % ctr['nslot']
            ctr['w'] = (s + 1) % ctr['nslot']
            buf = WS[s] if s < NSLOT else XS[s - NSLOT]
            views = []
            off = 0
            for src, kc, ncols in parts:
                view = buf[:, off:off + kc * ncols].rearrange("p (k n) -> p k n", k=kc)
                S.dma('pool', f'w{s}', view, src, writes=[('W', s)])
                views.append(view)
                off += kc * ncols
            return views, ('W', s)

        def wsrc(w, l, r0, nrows, c0, ncols):
            return w[l, r0:r0 + nrows, c0:c0 + ncols].rearrange("(k p) n -> p k n", p=128)

        def chain(outap, okey, pairs, reads):
            n = len(pairs)
            fns = []
            for i, (lh, rh) in enumerate(pairs):
                fns.append(lambda e, lh=lh, rh=rh, i=i: e.matmul(outap, lh, rh, start=(i == 0), stop=(i == n - 1)))
            S.op('pe', fns, reads=reads, writes=[okey])


        def ACT(out, in_, func, reads, writes, scale=None, bias=None):
            kw = {}
            if scale is not None:
                kw['scale'] = scale
            if bias is not None:
                kw['bias'] = bias
            S.op('act', lambda e: e.activation(out=out, in_=in_, func=func, **kw), reads=reads, writes=writes)

        def TT(out, in0, in1, op, reads, writes):
            S.op('dve', lambda e: e.tensor_tensor(out=out, in0=in0, in1=in1, op=op), reads=reads, writes=writes)

        def STT(out, in0, scalar, in1, op0, op1, reads, writes):
            S.op('dve', lambda e: e.scalar_tensor_tensor(out=out, in0=in0, scalar=scalar, in1=in1, op0=op0, op1=op1), reads=reads, writes=writes)

        def TSADD(out, in0, scalar1, reads, writes):
            S.op('dve', lambda e: e.tensor_scalar(out=out, in0=in0, scalar1=scalar1, scalar2=None, op0=ALU.add), reads=reads, writes=writes)

        def COPY(out, in_, reads, writes):
            S.op('dve', lambda e: e.tensor_copy(out=out, in_=in_), reads=reads, writes=writes)

        def RECIP(out, in_, reads, writes):
            S.op('dve', lambda e: e.reciprocal(out=out, in_=in_), reads=reads, writes=writes)

        def MM(out, lh, rh, start, stop, reads, writes):
            S.op('pe', lambda e: e.matmul(out, lh, rh, start=start, stop=stop), reads=reads, writes=writes)

        S.dma('sp', 'c0', CB[:], cbd[:, :], writes=['CB'])
        S.dma('sp', 'c1', SF[:], sfd[:, :], writes=['SF'])
        S.op('act', lambda e: e.activation(out=SCB[:], in_=CV, func=AF.Silu), reads=['SF'], writes=['SCB'])
        S.op('act', lambda e: e.activation(out=ESb[:], in_=SINK, func=AF.Exp), reads=['SF'], writes=['ES'])

        def modulation_steps(l):
            bk = PSB[7]
            bkeys = [('DE', 1, 0), ('DE', 1, 1), ('PS', 7)]
            psv = bk[:, 0:96].rearrange("p (j v) -> p j v", v=2)
            steps = []

            held = {}

            def mk(js):
                def step():
                    if js > 0:
                        wv, wk = held.pop(js - 1)
                        for jj in range(2):
                            j = (js - 1) * 2 + jj
                            chain(bk[:, j * 2:j * 2 + 2], bkeys[0],
                                  [(wv[:, kc, jj * 128:(jj + 1) * 128], SCB[:, kc, :]) for kc in range(8)],
                                  [wk, 'SCB'] + (bkeys[1:] if j == 0 else []))
                    if js < 24:
                        held[js] = wload(wsrc(wada, l, 0, 1024, js * 256, 256), 8, 256)
                return step
            for js in range(25):
                steps.append(mk(js))

            def fin():
                for v in range(2):
                    TT(MODT[:, l, :, v], psv[:, :, v], BADA[:, l, :], ALU.add, bkeys + ['SF'], [('MOD', l)])
                for which in range(2):
                    off = 8 if which == 0 else 32
                    for v in range(2):
                        STT(GS[:, l, which, :, v], MODT[:, l, off:off + 8, v], 1.0, G12[which][:, l, :], ALU.add, ALU.mult,
                            [('MOD', l), 'SF'], [('GS', l)])
                S.last_w[bkeys[1]] = S.last_w[bkeys[0]]
                S.last_w[bkeys[2]] = S.last_w[bkeys[0]]
            steps.append(fin)
            return steps

        def mSH(l, which, c, v): return MODT[:, l, (0 if which == 0 else 24) + c, v:v + 1]
        def mGA(l, which, c, v): return MODT[:, l, (16 if which == 0 else 40) + c, v:v + 1]
        def mGS(l, which, c, v): return GS[:, l, which, c, v:v + 1]

        def rstd_block(t0, lb):
            tb = t0 // 512 + lb
            tok = slice(t0 + lb * 512, t0 + lb * 512 + 512)
            bk, bkk = bank()
            for c2 in range(4):
                sqs = []
                for c in (2 * c2, 2 * c2 + 1):
                    sq, sqk = t16()
                    if c % 2 == 0:
                        ACT(sq[:], X[:, c, tok], AF.Square, [kX(c, tb)], [sqk])
                    else:
                        TT(sq[:], X[:, c, tok], X[:, c, tok], ALU.mult, [kX(c, tb)], [sqk])
                    sqs.append((c, sq, sqk))
                for c, sq, sqk in sqs:
                    MM(bk[:], ONESM, sq[:], c == 0, c == 7, [sqk, 'CB'], [bkk])
            ACT(RSTD[:], bk[:], AF.Ln, [bkk], ['RSTD'], scale=1.0, bias=EPS)
            ACT(RSTD[:], RSTD[:], AF.Exp, ['RSTD'], ['RSTD'], scale=-0.5)
            return tb, tok

        def norm(l, which, v, t0, nblk, dst, dkey):
            for lb in range(nblk):
                tb, tok = rstd_block(t0, lb)
                for c in range(8):
                    tm, tmk = tmp32()
                    TT(tm[:], X[:, c, tok], RSTD[:], ALU.mult, [kX(c, tb), 'RSTD'], [tmk])
                    ACT(dst[:, c, lb * 512:(lb + 1) * 512], tm[:], AF.Identity, [tmk, ('MOD', l), ('GS', l)], [dkey(c, lb)],
                        scale=mGS(l, which, c, v), bias=mSH(l, which, c, v))

        def rope(bk, bkk, dst, dkeys, lb):
            zb, zbk = t16()
            ACT(zb[:], bk[:], AF.Identity, [bkk], [zbk])
            flush()

            def rest():
                b2, b2k = bank()
                MM(b2[:], PSW, zb[:], True, True, [zbk, 'CB'], [b2k])
                t1, t1k = tmp32()
                TT(t1[:], bk[:], ROPE[:, 0, lb * 512:(lb + 1) * 512], ALU.mult, [bkk, 'ROPE', zbk], [t1k])
                t2, t2k = tmp32()
                TT(t2[:], b2[:], ROPE[:, 1, lb * 512:(lb + 1) * 512], ALU.mult, [b2k, 'ROPE'], [t2k])
                TT(dst, t1[:], t2[:], ALU.add, [t1k, t2k], dkeys)
            defer(rest)

        def phase1(l, v, t0, sample):
            norm(l, 0, v, t0, 2, HT, kHT)
            if sample:
                for k in range(2):
                    S.dma('sp', 'rope', ROPE[:, k, :], roped[k, :, t0:t0 + 1024], writes=['ROPE'])
            if P1STOP == 'norm':
                return
            for uh in range(2):
                wv, wk = wload(wsrc(W1, l, 0, 1024, uh * 256, 256), 8, 256)
                for tt in range(8):
                    gt = t0 // 128 + tt
                    bk, bkk = bank()
                    chain(bk[:, 0:256], bkk, [(HT[:, kc, tt * 128:(tt + 1) * 128], wv[:, kc, :]) for kc in range(8)],
                          [wk] + [kHT(c, tt // 4) for c in range(8)])
                    ACT(Uv[:, gt, uh * 256:(uh + 1) * 256], bk[:, 0:256], AF.Identity, [bkk], [kU(gt)])
            if P1STOP == 'U':
                return
            wvv, wvk = wload(wsrc(W1, l, 0, 1024, 512, 256), 8, 256)
            wkv, wkk = wload(wsrc(W1, l, 0, 1024, 768, 256), 8, 256)
            for tt in range(8):
                gt = t0 // 128 + tt
                hk = [kHT(c, tt // 4) for c in range(8)]
                bk, bkk = bank()
                chain(bk[:, 0:256], bkk, [(HT[:, kc, tt * 128:(tt + 1) * 128], wvv[:, kc, :]) for kc in range(8)], [wvk] + hk)
                if sample:
                    COPY(Vv[:, gt, :], bk[:, 0:256], [bkk], [kV(gt)])
                else:
                    b2, b2k = bank()
                    chain(b2[:, 0:256], b2k, [(HT[:, kc, tt * 128:(tt + 1) * 128], wkv[:, kc, :]) for kc in range(8)], [wkk] + hk)
                    st, stk = tmp32()
                    ACT(st[:, 0:256], bk[:, 0:256], AF.Identity, [bkk], [stk])
                    ACT(st[:, 256:512], b2[:, 0:256], AF.Identity, [b2k], [stk])
                    COPY(Vv[:, gt, :], st[:, 0:256], [stk], [kV(gt)])
                    b = tt // 2
                    s0 = (tt % 2) * 128
                    if P1STOP not in ('Vnodma', 'Vk'):
                        S.dma('sp', 'out', nvd[b, l, s0:s0 + 128, :], st[:, 0:256], reads=[stk], writes=[('OUT', 'nv', l, tt)])
                        S.dma('sp', 'out', nkd[b, l, s0:s0 + 128, :], st[:, 256:512], reads=[stk], writes=[('OUT', 'nk', l, tt)])
            if P1STOP in ('V', 'Vnodma', 'Vonly', 'Vk'):
                return
            for c in range(2):
                for lb in range(2):
                    tb = t0 // 512 + lb
                    bk, bkk = bank()
                    chain(bk[:], bkk, [(wkv[:, kc, c * 128:(c + 1) * 128], HT[:, kc, lb * 512:(lb + 1) * 512]) for kc in range(8)],
                          [wkk] + [kHT(cc, lb) for cc in range(8)])
                    dst = KTv[:, c, t0 + lb * 512:t0 + lb * 512 + 512]
                    if sample:
                        rope(bk, bkk, dst, [kKT(c, tb)], lb)
                    else:
                        COPY(dst, bk[:], [bkk], [kKT(c, tb)])
            flush()

        def fourier(l, t0, sample, hook=None):
            for qi in range(4):
                if qi == 1 and hook is not None:
                    hook()
                if sample:
                    qu = t0 // 256 + qi
                    slots = []
                    for k in range(2):
                        for sh in range(2):
                            slots.append(wload(dftd[k, qu, sh].rearrange("p (s n) -> p s n", s=8), 8, 256))
                    nst = 16
                    rcl = [(slots[st // 8][0][:, st % 8, :], slots[st // 8][1]) for st in range(16)]
                    rsl = [(slots[2 + st // 8][0][:, st % 8, :], slots[2 + st // 8][1]) for st in range(16)]
                    ut0 = 0
                else:
                    nst = 2
                    rcl = [(CPv[:, st, :], 'CB') for st in range(2)]
                    rsl = [(SPv[:, st, :], 'CB') for st in range(2)]
                    ut0 = qi * 2
                for g in range(4):
                    bk, bkk = bank()
                    prs = [(Uv[:, ut0 + st, g * 128:(g + 1) * 128], rcl[st][0]) for st in range(nst)]
                    rk = list({rcl[st][1] for st in range(nst)}) + [kU(ut0 + st) for st in range(nst)]
                    chain(bk[:, 0:256], bkk, prs, rk)
                    COPY(DC[g][:], bk[:, 0:256], [bkk], [('DC', g)])
                    if g == 0:
                        flush()
                for g in range(4):
                    bk, bkk = bank()
                    prs = [(Uv[:, ut0 + st, g * 128:(g + 1) * 128], rsl[st][0]) for st in range(nst)]
                    rk = list({rsl[st][1] for st in range(nst)}) + [kU(ut0 + st) for st in range(nst)]
                    chain(bk[:, 0:256], bkk, prs, rk)
                    COPY(DS[g % 2][:], bk[:, 0:256], [bkk], [('DS', g % 2)])
                    flush()

                    def rest(g=g, qi=qi):
                        b2, b2k = bank()
                        chain(b2[:, 0:256], b2k, [(C128, DC[g][:]), (NS128, DS[g % 2][:])], ['CB', ('DC', g), ('DS', g % 2)])
                        ACT(Fb[:, g, qi * 256:(qi + 1) * 256], b2[:, 0:256], AF.Identity, [b2k], [('Fq', g, qi)])
                    defer(rest)
            flush()

        def kFr(g, lb): return [('Fq', g, lb * 2), ('Fq', g, lb * 2 + 1)]

        def qnorm(l, v, t0):
            for k in range(2):
                S.dma('sp', 'rope', ROPE[:, k, :], roped[k, :, t0:t0 + 1024], writes=['ROPE'])
            norm(l, 0, v, t0, 2, HT, kHT)

        def qproj(l, v, t0, sample):
            for cs in range(4):
                wv, wk = wload(wsrc(W1, l, 0, 1024, 1024 + cs * 256, 256), 8, 256)
                for cc in range(2):
                    c = cs * 2 + cc
                    for lb in range(2):
                        bk, bkk = bank()
                        chain(bk[:], bkk, [(wv[:, kc, cc * 128:(cc + 1) * 128], HT[:, kc, lb * 512:(lb + 1) * 512]) for kc in range(8)],
                              [wk] + [kHT(k2, lb) for k2 in range(8)])
                        dst = Qv[:, c, lb * 512:(lb + 1) * 512]
                        if sample:
                            rope(bk, bkk, dst, [kQ(c, lb)], lb)
                        else:
                            ACT(dst, bk[:], AF.Identity, [bkk], [kQ(c, lb)])
            flush()

        actr = {'n': 0}
        att_pend = []

        def attention(l, t0, sample):
            ctr['ngen'] = 4
            for i in range(8):
                gi = t0 // 128 + i
                if sample:
                    keys = []
                    if gi > 0:
                        keys.append((0, gi - 1, MPREV))
                    keys.append((0, gi, None))
                    if gi < 15:
                        keys.append((0, gi + 1, MNEXT))
                    keys += [(1, 0, None), (1, 1, None)]
                else:
                    sq = i // 2
                    keys = [(0, 2 * sq, None), (0, 2 * sq + 1, None)]
                nk = len(keys)
                lb = i // 4
                for gp in range(2):
                    par = actr['n'] % 2
                    actr['n'] += 1
                    OBs = [PSB[4 + 2 * par], PSB[4 + 2 * par]]
                    DENs = [PSB[5 + 2 * par], PSB[5 + 2 * par]]
                    qk = {}

                    def emit_qk(idx):
                        src, kt, msk = keys[idx]
                        for gg in range(2):
                            pb = gg * 64
                            bk, bkk = bank()
                            if src == 0:
                                lh = KTv[pb:pb + 64, gp, kt * 128:(kt + 1) * 128]
                                lk = kKT(gp, kt // 4)
                            else:
                                lh = KC[pb:pb + 64, gp, kt * 128:(kt + 1) * 128]
                                lk = 'KC'
                            rh = Qv[pb:pb + 64, gp * 4:(gp + 1) * 4, i * 128:(i + 1) * 128]
                            MM(bk[:], lh, rh, True, True, [lk] + [kQ(gp * 4 + j, lb) for j in range(4)], [bkk])
                            qk[(idx, gg)] = (bk, bkk)

                    emit_qk(0)
                    if nk > 1:
                        emit_qk(1)
                    for idx in range(nk):
                        src, kt, msk = keys[idx]
                        pts = []
                        for gg in range(2):
                            bk, bkk = qk[(idx, gg)]
                            pt, ptk = ptbuf()
                            ACT(pt[:], bk[:], AF.Exp, [bkk], [ptk], scale=0.125)
                            if msk is not None:
                                TT(pt[:], pt[:], msk, ALU.mult, [ptk, 'CB'], [ptk])
                            pts.append((pt, ptk))
                        if idx == min(1, nk - 1):
                            while att_pend:
                                att_pend.pop(0)()
                        for gg in range(2):
                            g = gp * 2 + gg
                            pb = gg * 64
                            pt, ptk = pts[gg]
                            if src == 0:
                                vv = Vv[:, kt, g * 64:(g + 1) * 64]
                                vk = kV(kt)
                            else:
                                vv = VC[:, kt, g * 64:(g + 1) * 64]
                                vk = 'VC'
                            MM(OBs[gg][pb:pb + 64, :], vv, pt[:], idx == 0, idx == nk - 1, [vk, ptk], [('OB', par, gg), ('PS', 4 + 2 * par)])
                        for gg in range(2):
                            pb = gg * 64
                            pt, ptk = pts[gg]
                            MM(DENs[gg][pb:pb + 64, :], ONES1[:, 0:64], pt[:], idx == 0, idx == nk - 1, [ptk, 'CB'], [('DE', par, gg), ('PS', 5 + 2 * par)])
                        if idx + 2 < nk:
                            emit_qk(idx + 2)
                    def norm_att(OB=OBs[0], DEN=DENs[0], par=par, gp=gp, i=i, lb=lb):
                        ds_, dsk = tmp32()
                        for j in range(4):
                            TSADD(ds_[:, j * 128:(j + 1) * 128], DEN[:, j * 128:(j + 1) * 128],
                                  ESb[:, l * 8 + gp * 4 + j:l * 8 + gp * 4 + j + 1], [('DE', par, 0), ('DE', par, 1), ('PS', 5 + 2 * par), 'ES'], [dsk])
                        ACT(ds_[:], ds_[:], AF.Ln, [dsk], [dsk])
                        ACT(ds_[:], ds_[:], AF.Exp, [dsk], [dsk], scale=-1.0)
                        TT(ATTv[:, gp * 4:(gp + 1) * 4, i * 128:(i + 1) * 128],
                           OB[:, :].rearrange("p (j q) -> p j q", j=4),
                           ds_[:, :].rearrange("p (j q) -> p j q", j=4), ALU.mult,
                           [('OB', par, 0), ('OB', par, 1), ('PS', 4 + 2 * par), dsk],
                           [('ATTp', gp * 4 + j, lb, i) for j in range(4)] + [kATT(gp * 4 + j, lb) for j in range(4)])
                    att_pend.append(norm_att)
            while att_pend:
                att_pend.pop(0)()
            ctr['ngen'] = 7

        def kATTr(c, lb):
            return [('ATTp', c, lb, i) for i in range(lb * 4, lb * 4 + 4)] + [kATT(c, lb)]

        def merge(l, v, t0):
            for c in range(8):
                (fo, ao), fak = wload_multi([(wsrc(wfo, l, 0, 512, c * 128, 128), 4, 128),
                                             (wsrc(wao, l, 0, 1024, c * 128, 128), 8, 128)])
                (gf, ga), ggk = wload_multi([(wsrc(W1, l, 0, 1024, 2048 + c * 128, 128), 8, 128),
                                             (wsrc(W1, l, 0, 1024, 3072 + c * 128, 128), 8, 128)])
                for lb in range(2):
                    ts = slice(lb * 512, (lb + 1) * 512)
                    bgf, bgfk = bank()
                    chain(bgf[:], bgfk, [(gf[:, kc, :], HT[:, kc, ts]) for kc in range(8)], [ggk] + [kHT(k2, lb) for k2 in range(8)])
                    sf, sfk = tmp32()
                    ACT(sf[:], bgf[:], AF.Sigmoid, [bgfk], [sfk])
                    bfo, bfok = bank()
                    chain(bfo[:], bfok, [(fo[:, kc, :], Fb[:, kc, ts]) for kc in range(4)],
                          [fak] + [k for g in range(4) for k in kFr(g, lb)])
                    TT(sf[:], bfo[:], sf[:], ALU.mult, [bfok, sfk], [sfk])
                    bga, bgak = bank()
                    chain(bga[:], bgak, [(ga[:, kc, :], HT[:, kc, ts]) for kc in range(8)], [ggk] + [kHT(k2, lb) for k2 in range(8)])
                    sa, sak = tmp32()
                    ACT(sa[:], bga[:], AF.Sigmoid, [bgak], [sak])
                    bao, baok = bank()
                    chain(bao[:], baok, [(ao[:, kc, :], ATTv[:, kc, ts]) for kc in range(8)],
                          [fak] + [k for k2 in range(8) for k in kATTr(k2, lb)])
                    TT(sa[:], bao[:], sa[:], ALU.mult, [baok, sak], [sak])
                    TT(Qv[:, c, ts], sf[:], sa[:], ALU.add, [sfk, sak], [kQ(c, lb)])

        def wo(l, v, t0):
            for cp in range(4):
                wv, wk = wload(wsrc(wout, l, 0, 1024, cp * 256, 256), 8, 256)
                for cc in range(2):
                    c = cp * 2 + cc
                    for lb in range(2):
                        tb = t0 // 512 + lb
                        tok = slice(t0 + lb * 512, t0 + lb * 512 + 512)
                        bk, bkk = bank()
                        chain(bk[:], bkk, [(wv[:, kc, cc * 128:(cc + 1) * 128], Qv[:, kc, lb * 512:(lb + 1) * 512]) for kc in range(8)],
                              [wk] + [kQ(k2, lb) for k2 in range(8)])
                        STT(X[:, c, tok], bk[:], mGA(l, 0, c, v), X[:, c, tok], ALU.mult, ALU.add,
                            [bkk, kX(c, tb), ('MOD', l)], [kX(c, tb)])

        def mlp(l, v, nblk, hook=None):
            norm(l, 1, v, 0, nblk, H2T, kH2)
            for hg in range(8):
                ab = hg % 2
                for sh in range(2):
                    if hook is not None:
                        hook()
                    wv, wk = wload(wsrc(wff1, l, 0, 1024, hg * 512 + sh * 256, 256), 8, 256)
                    for cc in range(2):
                        kcl = sh * 2 + cc
                        for lb in range(nblk):
                            ts = slice(lb * 512, (lb + 1) * 512)
                            bk, bkk = bank()
                            chain(bk[:], bkk, [(wv[:, kc, cc * 128:(cc + 1) * 128], H2T[:, kc, ts]) for kc in range(8)],
                                  [wk] + [kH2(k2, lb) for k2 in range(8)])
                            tm, tmk = tmp32()
                            ACT(tm[:], bk[:], AF.Relu, [bkk], [tmk])
                            TT(ATv[ab][:, kcl, ts], tm[:], tm[:], ALU.mult, [tmk], [kAT(ab, kcl, lb)])
                for oh in range(2):
                    if hook is not None:
                        hook()
                    wv, wk = wload(wsrc(wff2, l, hg * 512, 512, oh * 512, 512), 4, 512)
                    for cc in range(4):
                        c = oh * 4 + cc
                        for lb in range(nblk):
                            ts = slice(lb * 512, (lb + 1) * 512)
                            bk, bkk = bank()
                            chain(bk[:], bkk, [(wv[:, kc, cc * 128:(cc + 1) * 128], ATv[ab][:, kc, ts]) for kc in range(4)],
                                  [wk] + [kAT(ab, k2, lb) for k2 in range(4)])
                            STT(X[:, c, ts], bk[:], mGA(l, 1, c, v), X[:, c, ts], ALU.mult, ALU.add,
                                [bkk, kX(c, lb), ('MOD', l)], [kX(c, lb)])

        def final(nblk, outd):
            for lb in range(nblk):
                tb, tok = rstd_block(0, lb)
                for c in range(8):
                    tm, tmk = tmp32()
                    TT(tm[:], X[:, c, tok], RSTD[:], ALU.mult, [kX(c, tb), 'RSTD'], [tmk])
                    ACT(tm[:], tm[:], AF.Identity, [tmk, 'SF'], [tmk], scale=FG[:, c:c + 1])
                    S.dma('sp', 'out', outd[c * 128:(c + 1) * 128, tok], tm[:], reads=[tmk], writes=[('OUT', 'y', c, lb)])

        def load_x(xd, T):
            for c in range(8):
                extra = [('W', NSLOT + c)] if T > 1024 else []
                S.dma('sp', 'x', X[:, c, 0:T], xd[c * 128:(c + 1) * 128, :], writes=[kX(c, tb) for tb in range(T // 512)] + extra)

        if do_prompt:
            load_x(xpT, 1024)
            ctr['nslot'] = NSLOT + 8
        for l in range(n_layers):
            if l == 0 or not do_prompt:
                for st in modulation_steps(l):
                    st()
            nxt = modulation_steps(l + 1) if (do_prompt and l + 1 < n_layers) else []

            def mhook(nxt=nxt):
                if len(nxt) > 1:
                    nxt.pop(0)()
            if do_prompt:
                if _on('p1'): phase1(l, 0, 0, False)
                if _on('fourier'): fourier(l, 0, False)
                if _on('qproj'): qproj(l, 0, 0, False)
                if _on('attn'): attention(l, 0, False)
                if _on('merge'): merge(l, 0, 0)
                if _on('wo'): wo(l, 0, 0)
                if _on('mlp'): mlp(l, 0, 2, hook=mhook)
            while nxt:
                nxt.pop(0)()
        if mode == 'io':
          for rep in range(10):
            for c in range(8):
                S.dma('sp', 'out', ypT[c * 128:(c + 1) * 128, :], X[:, c, 0:1024], reads=[kX(c, 0), kX(c, 1)], writes=[('OUT', c, rep)])
        elif mode == 'sq':
            for c in range(8):
                tm, tmk = tmp32()
                ACT(tm[:], X[:, c, 0:512], AF.Square, [kX(c, 0)], [tmk])
                S.dma('sp', 'out', ypT[c * 128:(c + 1) * 128, 0:512], tm[:], reads=[tmk], writes=[('OUT', c)])
        elif do_prompt:
            final(2, ypT)
        ctr['nslot'] = NSLOT
        ctr['w'] = 0
        if do_sample:
            load_x(xsT, 2048)
            for l in range(n_layers):
                S.dma('pool', 'kc', KC[:], kcT[l].rearrange("(c p) t -> p c t", p=128), writes=['KC'])
                S.dma('pool', 'vc', VC[:], vcd[l].rearrange("(s p) n -> p s n", p=128), writes=['VC'])
                for hf in range(2):
                    if _on('p1'): phase1(l, 1, hf * 1024, True)
                for hf in range(2):
                    t0 = hf * 1024
                    if _on('fourier'): fourier(l, t0, True, hook=(lambda l=l, t0=t0: qnorm(l, 1, t0)))
                    if _on('qproj'): qproj(l, 1, t0, True)
                    if _on('attn'): attention(l, t0, True)
                    if _on('merge'): merge(l, 1, t0)
                    if _on('wo'): wo(l, 1, t0)
                if _on('mlp'): mlp(l, 1, 4)
            final(4, ysT)
        S.wait_all('sp', 'out')

        @block.sync
        def _(e):
            S.replay('sp', e)

        @block.scalar
        def _(e):
            S.replay('act', e)

        @block.vector
        def _(e):
            S.replay('dve', e)

        @block.gpsimd
        def _(e):
            S.replay('pool', e)

        @block.tensor
        def _(e):
            S.replay('pe', e)
    return nc, S


_CONST = None


def _prep(inputs):
    global _CONST
    if _CONST is None:
        _CONST = _const_tables()
    cb, dfts, rope = _CONST
    f = lambda a: np.ascontiguousarray(np.asarray(a, dtype=np.float32))
    w_in = f(inputs['w_in'])
    hp = _head_perm()
    qcols = np.concatenate([np.arange(512 + h * 64, 512 + (h + 1) * 64) for h in hp])
    W1 = np.ascontiguousarray(np.concatenate(
        [w_in[:, :, 0:512], w_in[:, :, 1792:2048], w_in[:, :, 1536:1792], w_in[:, :, qcols], w_in[:, :, 2048:4096]], axis=2))
    wao = np.ascontiguousarray(f(inputs['w_ao'])[:, qcols - 512, :])
    shared = dict(W1=W1, wada=f(inputs['w_ada']), wfo=f(inputs['w_fo']), wao=wao, wout=f(inputs['w_out']),
                  wff1=f(inputs['w_ff1']), wff2=f(inputs['w_ff2']), cb=cb, dfts=dfts, rope=rope)
    x_prompt = f(inputs['x_prompt'])
    x_sample = f(inputs['x_sample'])
    c = f(inputs['c'])
    c_ctx = f(inputs['c_ctx'])
    ck = f(inputs['cache_k'])
    cv = f(inputs['cache_v'])
    b_ada = f(inputs['b_ada'])
    sfbase = np.zeros((128, SF_N), np.float32)
    sfbase[:, SF_BADA:SF_BADA + 192] = b_ada.reshape(4, 48, 128).transpose(2, 0, 1).reshape(128, 192)
    sfbase[:, SF_G1:SF_G1 + 32] = f(inputs['norm1_g']).reshape(4, 8, 128).transpose(2, 0, 1).reshape(128, 32)
    sfbase[:, SF_G2:SF_G2 + 32] = f(inputs['norm2_g']).reshape(4, 8, 128).transpose(2, 0, 1).reshape(128, 32)
    sfbase[:, SF_FG:SF_FG + 8] = f(inputs['final_g']).reshape(8, 128).T
    snk = f(inputs['sink']).reshape(4, 2, 2, 4)
    sfbase[0:64, SF_SINK:SF_SINK + 32] = np.broadcast_to(snk[:, :, 0, :].reshape(1, 32), (64, 32))
    sfbase[64:128, SF_SINK:SF_SINK + 32] = np.broadcast_to(snk[:, :, 1, :].reshape(1, 32), (64, 32))
    in_maps = []
    for i in range(8):
        m = dict(shared)
        m['xpT'] = np.ascontiguousarray(x_prompt[4 * i:4 * i + 4].reshape(1024, 1024).T)
        m['xsT'] = np.ascontiguousarray(x_sample[i].T)
        m['kcT'] = np.ascontiguousarray(ck[i].reshape(4, 256, 256).transpose(0, 2, 1))
        m['vc'] = np.ascontiguousarray(cv[i].reshape(4, 256, 256))
        sfa = sfbase.copy()
        cvv = np.stack([c_ctx, c[i]], axis=0)
        sfa[:, SF_CV:SF_CV + 16] = cvv.reshape(2, 8, 128).transpose(2, 1, 0).reshape(128, 16)
        m['sf'] = sfa
        in_maps.append(m)
    return in_maps


_NC = None


def kernel(**inputs):
    global _NC
    in_maps = _prep(inputs)
    if _NC is None:
        _NC = build()[0]
    res = run_bass_kernel_spmd(_NC, in_maps, core_ids=list(range(8)))
    y_prompt = np.zeros((32, 256, 1024), np.float32)
    y_sample = np.zeros((8, 2048, 1024), np.float32)
    nk = np.zeros((32, 4, 256, 4, 64), np.float32)
    nv = np.zeros((32, 4, 256, 4, 64), np.float32)
    for i in range(8):
        r = res.results[i]
        y_prompt[4 * i:4 * i + 4] = r['ypT'].T.reshape(4, 256, 1024)
        y_sample[i] = r['ysT'].T
        nk[4 * i:4 * i + 4] = r['nk'].reshape(4, 4, 256, 4, 64)
        nv[4 * i:4 * i + 4] = r['nv'].reshape(4, 4, 256, 4, 64)
    return (y_prompt, y_sample, nk, nv)
```
